# Optimizing a Trainium2 kernel written in Bass

```python
import jax, jax.numpy as jnp
from jax import lax
import numpy as np

D_MODEL = 1024
BATCH = 2
SEQ = 16384
DEPTH = 4

GRID_W = 64
CTX_LEN = 256
D_MIX = D_MODEL
HEAD_DIM = 64
CONV_CH = D_MIX // 4
CONV_K = 3
ATT_HEADS = D_MIX // 2 // HEAD_DIM
ATT_KV_HEADS = 2
ATT_GROUP = ATT_HEADS // ATT_KV_HEADS
WINDOW = 128
BLOCK = 128
GLA_HEADS = D_MIX // 4 // HEAD_DIM
GLA_DK = HEAD_DIM
GLA_DV = HEAD_DIM
GLA_RANK = 16
GLA_NORMALIZER = 16.0
GLA_CHUNK = 32
D_FF = ((8 * D_MODEL + 3 * 256 - 1) // (3 * 256)) * 256
N_MOD = 6
ROPE_BASE = 10000.0
EPS = 1e-6
COL_SIZES = (CONV_CH, CONV_CH, CONV_CH,
             ATT_HEADS * HEAD_DIM, ATT_KV_HEADS * HEAD_DIM, ATT_KV_HEADS * HEAD_DIM,
             GLA_HEADS * GLA_DK, GLA_HEADS * GLA_DK, GLA_HEADS * GLA_DV, GLA_HEADS * GLA_DV,
             2 * GLA_RANK)
N_IN = sum(COL_SIZES)
COL_SPLITS = tuple(int(v) for v in np.cumsum(COL_SIZES)[:-1])

kernel_name = "hybrid_parallel_groups_dit"


def rmsnorm(x, g):
    x32 = x.astype(jnp.float32)
    y = x32 * lax.rsqrt(jnp.mean(x32 * x32, axis=-1, keepdims=True) + EPS)
    return (y * g.astype(jnp.float32)).astype(x.dtype)


def heads(a, n):
    return a.reshape(a.shape[:-1] + (n, a.shape[-1] // n))


def axial_rope_tables(rows):
    t = jnp.arange(rows * GRID_W)
    row = (t // GRID_W).astype(jnp.float32)
    col = (t % GRID_W).astype(jnp.float32)
    n_freq = HEAD_DIM // 4
    inv_freq = ROPE_BASE ** (-jnp.arange(n_freq, dtype=jnp.float32) / n_freq)
    ang = jnp.stack([row[:, None] * inv_freq, col[:, None] * inv_freq], axis=1)
    return jnp.cos(ang), jnp.sin(ang)


def apply_rope(x, cos, sin):
    B, S, H, D = x.shape
    xs = x.astype(jnp.float32).reshape(B, S, H, 2, D // 2)
    half = D // 4
    x1, x2 = xs[..., :half], xs[..., half:]
    c = cos[None, :, None]
    s = sin[None, :, None]
    out = jnp.concatenate([x1 * c - x2 * s, x2 * c + x1 * s], axis=-1)
    return out.reshape(B, S, H, D).astype(x.dtype)


def short_conv(x_in, b_gate, c_gate, w):
    T = x_in.shape[1]
    u = c_gate * x_in
    up = jnp.pad(u, ((0, 0), (CONV_K // 2, CONV_K // 2), (0, 0)))
    y = sum(w[j] * up[:, j:j + T] for j in range(CONV_K))
    return b_gate * y


def softmax_with_sink(scores, sink):
    s = jnp.broadcast_to(sink.astype(jnp.float32).reshape(ATT_KV_HEADS, ATT_GROUP, 1, 1),
                         scores.shape[:-1] + (1,))
    p = jax.nn.softmax(jnp.concatenate([s, scores], axis=-1), axis=-1)
    return p[..., 1:]


def context_attention(q, k, v, sink):
    B, L = q.shape[:2]
    qg = q.reshape(B, L, ATT_KV_HEADS, ATT_GROUP, HEAD_DIM)
    s = jnp.einsum('bqhgd,bkhd->bhgqk', qg, k).astype(jnp.float32) * HEAD_DIM ** -0.5
    p = softmax_with_sink(s, sink).astype(v.dtype)
    o = jnp.einsum('bhgqk,bkhd->bqhgd', p, v)
    return o.reshape(B, L, ATT_HEADS * HEAD_DIM)


def window_attention(q, k, v, k_ctx, v_ctx, sink):
    B, S = q.shape[:2]
    L = k_ctx.shape[1]
    nb = S // BLOCK
    qb = q.reshape(B, nb, BLOCK, ATT_KV_HEADS, ATT_GROUP, HEAD_DIM).transpose(1, 0, 2, 3, 4, 5)
    pad = ((0, 0), (BLOCK, BLOCK), (0, 0), (0, 0))
    kp = jnp.pad(k, pad)
    vp = jnp.pad(v, pad)
    r = jnp.arange(BLOCK)
    j = jnp.arange(3 * BLOCK)
    scale = HEAD_DIM ** -0.5

    def one_block(args):
        qi, i = args
        start = i * BLOCK
        ks = lax.dynamic_slice_in_dim(kp, start, 3 * BLOCK, axis=1)
        vs = lax.dynamic_slice_in_dim(vp, start, 3 * BLOCK, axis=1)
        s_ctx = jnp.einsum('bqhgd,bkhd->bhgqk', qi, k_ctx).astype(jnp.float32)
        s_loc = jnp.einsum('bqhgd,bkhd->bhgqk', qi, ks).astype(jnp.float32)
        tq = start + r
        tk = start - BLOCK + j
        valid = (jnp.abs(tq[:, None] - tk[None, :]) <= WINDOW) & (tk[None, :] >= 0) & (tk[None, :] < S)
        s_loc = jnp.where(valid, s_loc, -jnp.inf)
        p = softmax_with_sink(jnp.concatenate([s_ctx, s_loc], axis=-1) * scale, sink).astype(v.dtype)
        o = (jnp.einsum('bhgqk,bkhd->bqhgd', p[..., :L], v_ctx)
             + jnp.einsum('bhgqk,bkhd->bqhgd', p[..., L:], vs))
        return o.reshape(B, BLOCK, ATT_HEADS * HEAD_DIM)

    out = lax.map(one_block, (qb, jnp.arange(nb)))
    return out.transpose(1, 0, 2, 3).reshape(B, S, ATT_HEADS * HEAD_DIM)


def gla_chunk_scan(q, k, v, log_g, state0, with_output):
    B, T, H, _ = q.shape
    n = T // GLA_CHUNK

    def to_chunks(a):
        return a.reshape(B, n, GLA_CHUNK, H, a.shape[-1]).transpose(1, 0, 3, 2, 4)

    lower_tri = jnp.tril(jnp.ones((GLA_CHUNK, GLA_CHUNK), dtype=bool))[:, :, None]

    def step(state, inp):
        qc, kc, vc, gc = inp
        b = jnp.cumsum(gc.astype(jnp.float32), axis=2)
        b_last = b[:, :, -1:, :]
        new_state = (jnp.exp(b_last[:, :, 0, :])[..., None] * state
                     + jnp.einsum('bhsk,bhsv->bhkv', kc * jnp.exp(b_last - b), vc))
        if not with_output:
            return new_state, None
        diff = jnp.where(lower_tri, b[:, :, :, None, :] - b[:, :, None, :, :], -jnp.inf)
        a = jnp.einsum('bhtk,bhsk,bhtsk->bhts', qc, kc, jnp.exp(diff))
        o = (jnp.einsum('bhts,bhsv->bhtv', a, vc)
             + jnp.einsum('bhtk,bhkv->bhtv', qc * jnp.exp(b), state))
        return new_state, o

    state, o = lax.scan(step, state0, (to_chunks(q), to_chunks(k), to_chunks(v), to_chunks(log_g)))
    if not with_output:
        return None, state
    return o.transpose(1, 0, 3, 2, 4).reshape(B, T, H, v.shape[-1]), state


def gla_output(o, g, norm_g):
    o32 = o.astype(jnp.float32)
    o32 = o32 * lax.rsqrt(jnp.mean(o32 * o32, axis=-1, keepdims=True) + EPS) * norm_g.astype(jnp.float32)
    B, T = o.shape[:2]
    return (o32.reshape(B, T, GLA_HEADS * GLA_DV) * jax.nn.silu(g.astype(jnp.float32))).astype(g.dtype)


def gla_bidir(q, k, v, lr, g, q_c, k_c, v_c, lr_c, g_c, w_gate, b_gate, norm_g, ctx_out):
    B = q.shape[0]
    q = heads(q, GLA_HEADS) * GLA_DK ** -0.5
    k = heads(k, GLA_HEADS)
    v = heads(v, GLA_HEADS)
    q_c = heads(q_c, GLA_HEADS) * GLA_DK ** -0.5
    k_c = heads(k_c, GLA_HEADS)
    v_c = heads(v_c, GLA_HEADS)
    outs_x, outs_c = [], []
    for d in range(2):
        sl = slice(d * GLA_RANK, (d + 1) * GLA_RANK)
        gate_x = heads(jax.nn.log_sigmoid(lr[..., sl] @ w_gate[d] + b_gate[d]) / GLA_NORMALIZER, GLA_HEADS)
        gate_c = heads(jax.nn.log_sigmoid(lr_c[..., sl] @ w_gate[d] + b_gate[d]) / GLA_NORMALIZER, GLA_HEADS)
        flip = (lambda a: jnp.flip(a, axis=1)) if d == 1 else (lambda a: a)
        state0 = jnp.zeros((B, GLA_HEADS, GLA_DK, GLA_DV), jnp.float32)
        o_c, s_c = gla_chunk_scan(flip(q_c), flip(k_c), flip(v_c), flip(gate_c), state0, ctx_out)
        o_x, _ = gla_chunk_scan(flip(q), flip(k), flip(v), flip(gate_x), s_c, True)
        outs_x.append(flip(o_x))
        if ctx_out:
            outs_c.append(flip(o_c))
    y_x = gla_output(outs_x[0] + outs_x[1], g, norm_g)
    y_c = gla_output(outs_c[0] + outs_c[1], g_c, norm_g) if ctx_out else None
    return y_x, y_c


def swiglu(h, w_up, w_down):
    a, b = jnp.split(h @ w_up, 2, axis=-1)
    return (jax.nn.silu(a) * b) @ w_down


def layer(x, ctx, mod_x, mod_c, cos, sin, g1, g2, w_in, conv_w, sink, gate_w, gate_b, norm_g,
          w_out, w_up, w_down, ctx_out):
    sh1, sc1, gt1, sh2, sc2, gt2 = jnp.split(mod_x[:, None, :], N_MOD, axis=-1)
    csh1, csc1, cgt1, csh2, csc2, cgt2 = jnp.split(mod_c, N_MOD, axis=-1)
    hx = rmsnorm(x, g1) * (1 + sc1) + sh1
    hc = rmsnorm(ctx, g1) * (1 + csc1) + csh1
    (xc_in, xc_b, xc_c, xq, xk, xv, xgq, xgk, xgv, xgg, xlr) = jnp.split(hx @ w_in, COL_SPLITS, axis=-1)
    (cc_in, cc_b, cc_c, cq, ck, cv, cgq, cgk, cgv, cgg, clr) = jnp.split(hc @ w_in, COL_SPLITS, axis=-1)
    k_ctx = heads(ck, ATT_KV_HEADS)
    v_ctx = heads(cv, ATT_KV_HEADS)
    conv_x = short_conv(xc_in, xc_b, xc_c, conv_w)
    attn_x = window_attention(apply_rope(heads(xq, ATT_HEADS), cos, sin),
                              apply_rope(heads(xk, ATT_KV_HEADS), cos, sin),
                              heads(xv, ATT_KV_HEADS), k_ctx, v_ctx, sink)
    gla_x, gla_c = gla_bidir(xgq, xgk, xgv, xlr, xgg, cgq, cgk, cgv, clr, cgg,
                             gate_w, gate_b, norm_g, ctx_out)
    x = x + gt1 * (jnp.concatenate([conv_x, attn_x, gla_x], axis=-1) @ w_out)
    x = x + gt2 * swiglu(rmsnorm(x, g2) * (1 + sc2) + sh2, w_up, w_down)
    if ctx_out:
        conv_c = short_conv(cc_in, cc_b, cc_c, conv_w)
        attn_c = context_attention(heads(cq, ATT_HEADS), k_ctx, v_ctx, sink)
        ctx = ctx + cgt1 * (jnp.concatenate([conv_c, attn_c, gla_c], axis=-1) @ w_out)
        ctx = ctx + cgt2 * swiglu(rmsnorm(ctx, g2) * (1 + csc2) + csh2, w_up, w_down)
    return x, ctx


def setup_inputs(seed: int = 0) -> dict:
    key = jax.random.key(seed)
    ks = jax.random.split(key, 18)

    def nrm(k, shape, s):
        return jax.random.normal(k, shape, jnp.float32) * s

    return {
        "x": nrm(ks[0], (BATCH, SEQ, D_MODEL), 1.0),
        "c": nrm(ks[1], (BATCH, D_MODEL), 1.0),
        "ctx": nrm(ks[2], (BATCH, CTX_LEN, D_MODEL), 1.0),
        "c_ctx": nrm(ks[3], (D_MODEL,), 1.0),
        "w_mod": nrm(ks[4], (DEPTH, D_MODEL, N_MOD * D_MODEL), 0.02),
        "b_mod": nrm(ks[5], (DEPTH, N_MOD * D_MODEL), 0.01),
        "norm1_g": 1.0 + nrm(ks[6], (DEPTH, D_MODEL), 0.02),
        "norm2_g": 1.0 + nrm(ks[7], (DEPTH, D_MODEL), 0.02),
        "w_in": nrm(ks[8], (DEPTH, D_MODEL, N_IN), D_MODEL ** -0.5),
        "conv_w": nrm(ks[9], (DEPTH, CONV_K, CONV_CH), CONV_K ** -0.5),
        "attn_sink": nrm(ks[10], (DEPTH, ATT_HEADS), 1.0),
        "gla_gate_w": nrm(ks[11], (DEPTH, 2, GLA_RANK, GLA_HEADS * GLA_DK), GLA_RANK ** -0.5),
        "gla_gate_b": nrm(ks[12], (DEPTH, 2, GLA_HEADS * GLA_DK), 0.1),
        "gla_norm_g": 1.0 + nrm(ks[13], (DEPTH, GLA_DV), 0.02),
        "w_out": nrm(ks[14], (DEPTH, D_MIX, D_MODEL), D_MIX ** -0.5),
        "w_up": nrm(ks[15], (DEPTH, D_MODEL, 2 * D_FF), D_MODEL ** -0.5),
        "w_down": nrm(ks[16], (DEPTH, D_FF, D_MODEL), D_FF ** -0.5),
        "final_norm_g": 1.0 + nrm(ks[17], (D_MODEL,), 0.02),
    }


def reference(x, c, ctx, c_ctx, w_mod, b_mod, norm1_g, norm2_g, w_in, conv_w, attn_sink,
              gla_gate_w, gla_gate_b, gla_norm_g, w_out, w_up, w_down, final_norm_g):
    ROWS = x.shape[1] // GRID_W
    cos, sin = axial_rope_tables(ROWS)
    for l in range(DEPTH):
        mod_x = jax.nn.silu(c) @ w_mod[l] + b_mod[l]
        mod_c = jax.nn.silu(c_ctx) @ w_mod[l] + b_mod[l]
        x, ctx = layer(x, ctx, mod_x, mod_c, cos, sin, norm1_g[l], norm2_g[l], w_in[l], conv_w[l],
                       attn_sink[l], gla_gate_w[l], gla_gate_b[l], gla_norm_g[l], w_out[l],
                       w_up[l], w_down[l], l < DEPTH - 1)
    return rmsnorm(x, final_norm_g)
```

```python
import numpy as np
from contextlib import ExitStack
import concourse.bass as bass
import concourse.mybir as mybir
from concourse.bass_utils import run_bass_kernel_spmd

F32 = mybir.dt.float32
BF16 = mybir.dt.bfloat16
AF = mybir.ActivationFunctionType
ALU = mybir.AluOpType
AX = mybir.AxisListType

D = 1024
SEQ = 16384
NB = 2
DEPTH = 4
NSEG = 4
NLAT = SEQ // NSEG
NCTX = 256
NT = NLAT + NCTX
NCH = NT // 128
DFF = 2816
EPS = 1e-6
NFM = 3104
NTM = 768
C_U, C_B, C_Q, C_K, C_GQ, C_GK, C_GG = 0, 2, 4, 8, 10, 12, 14
NFMO = 16


class Ev:
    __slots__ = ("kind", "src", "val")

    def __init__(self, kind, src, val):
        self.kind = kind
        self.src = src
        self.val = val


class EngState:
    def __init__(self, prog, name, eng):
        self.prog = prog
        self.name = name
        self.eng = eng
        self.sem = prog.new_sem("e_" + name)
        self.seq = 0
        self.count = 0
        self.marks = []
        self.last = None
        self.last_marked = True
        self.seen = {}

    def mark_last(self):
        if not self.last_marked and self.last is not None:
            self.last.then_inc(self.sem, 1)
            self.count += 1
            self.marks.append((self.seq, self.count))
            self.last_marked = True

    def value_for(self, seq):
        ms = self.marks
        lo, hi = 0, len(ms)
        while lo < hi:
            mid = (lo + hi) // 2
            if ms[mid][0] >= seq:
                hi = mid
            else:
                lo = mid + 1
        if lo == len(ms):
            self.mark_last()
            return self.marks[-1][1]
        return ms[lo][1]

    def wait(self, ev):
        if ev is None:
            return
        if ev.kind == "eng":
            src = ev.src
            if ev.val <= 0:
                return
            val = src.value_for(ev.val)
            sem = src.sem
        else:
            sem = ev.src.sem
            val = ev.val
        key = id(sem)
        if self.seen.get(key, 0) >= val:
            return
        self.eng.wait_ge(sem, val)
        self.seen[key] = val


class DmaSem:
    def __init__(self, prog, name):
        self.sem = prog.new_sem("d_" + name)
        self.count = 0


class Tile:
    def __init__(self, prog, name, t=None):
        self.prog = prog
        self.name = name
        self.t = t
        self.w = None
        self.r = {}
        self.dsem = None
        self.is_dram = False

    def __getitem__(self, idx):
        return self.t[idx]

    def get_dsem(self):
        if self.dsem is None:
            self.dsem = DmaSem(self.prog, self.name)
            self.prog.dsems.append(self.dsem)
        return self.dsem


class Ring:
    def __init__(self, tiles):
        self.tiles = tiles
        self.i = 0

    def next(self):
        t = self.tiles[self.i % len(self.tiles)]
        self.i += 1
        return t


class Prog:
    def __init__(self, nc, stack):
        self.nc = nc
        self.stack = stack
        self.nsem = 0
        self.dsems = []
        self.pe = EngState(self, "pe", nc.tensor)
        self.act = EngState(self, "act", nc.scalar)
        self.dve = EngState(self, "dve", nc.vector)
        self.pool = EngState(self, "pool", nc.gpsimd)
        self.sp = EngState(self, "sp", nc.sync)
        self.engs = [self.pe, self.act, self.dve, self.pool, self.sp]

    def new_sem(self, name):
        self.nsem += 1
        return self.stack.enter_context(self.nc.semaphore(name + "_%d" % self.nsem))

    def sbuf(self, name, shape, dt, stack=None):
        st = stack if stack is not None else self.stack
        t = st.enter_context(self.nc.sbuf_tensor(name, list(shape), dt))
        return Tile(self, name, t)

    def ring(self, name, n, shape, dt, stack=None):
        return Ring([self.sbuf("%s%d" % (name, i), shape, dt, stack) for i in range(n)])

    def psum(self, name, shape, dt, stack=None):
        st = stack if stack is not None else self.stack
        t = st.enter_context(self.nc.psum_tensor(name, list(shape), dt))
        return Tile(self, name, t)

    def dram(self, name, shape, dt, kind="Internal"):
        t = self.nc.dram_tensor(name, list(shape), dt, kind=kind)
        tl = Tile(self, name, t.ap())
        tl.is_dram = True
        return tl

    def op(self, E, fn, reads=(), writes=(), mark=True):
        for t in reads:
            if t.w is not None:
                E.wait(t.w)
        for t in writes:
            if t.w is not None:
                E.wait(t.w)
            for ev in t.r.values():
                E.wait(ev)
        inst = fn(E.eng)
        E.seq += 1
        E.last = inst
        E.last_marked = False
        if mark:
            E.mark_last()
        ev = Ev("eng", E, E.seq)
        for t in reads:
            t.r[id(E)] = ev
        for t in writes:
            t.w = ev
            t.r = {}
        return inst

    def dma(self, Q, out_ap, in_ap, out_tile=None, in_tile=None, group=None, **kw):
        if in_tile is not None and in_tile.w is not None:
            Q.wait(in_tile.w)
        if out_tile is not None and not out_tile.is_dram:
            if out_tile.w is not None:
                Q.wait(out_tile.w)
            for ev in out_tile.r.values():
                Q.wait(ev)
        g = group if group is not None else out_tile
        ds = g.get_dsem()
        inst = Q.eng.dma_start(out=out_ap, in_=in_ap, **kw)
        inst.then_inc(ds.sem, 16)
        ds.count += 16
        Q.seq += 1
        Q.last = inst
        Q.last_marked = True
        ev = Ev("dma", ds, ds.count)
        if in_tile is not None:
            in_tile.r[id(ds)] = ev
        if out_tile is not None:
            out_tile.w = ev
            out_tile.r = {}
        return inst

    def barrier(self):
        for E in self.engs:
            E.mark_last()
        for F in self.engs:
            for E in self.engs:
                if E is F or E.count == 0:
                    continue
                if F.seen.get(id(E.sem), 0) < E.count:
                    F.eng.wait_ge(E.sem, E.count)
                    F.seen[id(E.sem)] = E.count
            for ds in self.dsems:
                if ds.count and F.seen.get(id(ds.sem), 0) < ds.count:
                    F.eng.wait_ge(ds.sem, ds.count)
                    F.seen[id(ds.sem)] = ds.count


def macro_tiles():
    mts = [(i * 512, 512, False) for i in range(NLAT // 512)]
    mts.append((NLAT, NCTX, True))
    return mts


class Builder:
    def __init__(self, phases, last=False):
        self.nc = bass.Bass("TRN2", target_bir_lowering=False)
        self.phases = phases
        self.last = last
        self.io = {}

    def din(self, name, shape, dt=F32):
        if name in self.io:
            return self.io[name]
        t = self.P.dram(name, shape, dt, kind="ExternalInput")
        self.io[name] = t
        return t

    def dout(self, name, shape, dt=F32):
        t = self.P.dram(name, shape, dt, kind="ExternalOutput")
        self.io[name] = t
        return t

    def dio(self, name, shape, dt, is_in, is_out):
        if name in self.io:
            return self.io[name]
        if is_in:
            return self.din(name, shape, dt)
        if is_out:
            return self.dout(name, shape, dt)
        t = self.P.dram(name, shape, dt)
        self.io[name] = t
        return t

    def build(self):
        nc = self.nc
        with ExitStack() as st:
            self.P = P = Prog(nc, st)
            self.consts(st)
            for ph in self.phases:
                with ExitStack() as pst, nc.named_scope("ph_" + ph):
                    self.pst = pst
                    getattr(self, "phase_" + ph)()
                    P.barrier()
            P.barrier()
        return nc

    def consts(self, st):
        P = self.P
        self.c_ident = self.din("c_ident", [128, 128])
        self.ident = P.sbuf("ident", [128, 128], BF16)
        P.dma(P.pool, self.ident[:], self.c_ident[:], out_tile=self.ident)
        self.psb = [P.psum("psb%d" % i, [128, 512], F32) for i in range(7)]
        self.pstp = P.psum("pstp", [128, 1024], BF16)

    def phase_M(self):
        P, pst = self.P, self.pst
        cT = self.din("cT", [128, 8, 2])
        wmod = self.din("w_mod", [DEPTH, D, 6 * D])
        bmod = self.din("b_mod2", [DEPTH, 2, 6 * D])
        modrow = self.dio("modrow", [DEPTH, 2, 6 * D], F32, False, "A" not in self.phases)
        cs = P.sbuf("m_cs", [128, 8, 2], F32, pst)
        sg = P.sbuf("m_sg", [128, 8, 2], F32, pst)
        bm = P.sbuf("m_bm", [2, 6 * D], F32, pst)
        mr = P.sbuf("m_mr", [2, 6 * D], F32, pst)
        wr = P.ring("m_w", 3, [128, 8, 512], F32, pst)
        P.dma(P.sp, cs[:], cT[:], out_tile=cs)
        P.op(P.act, lambda e: e.activation(out=sg[:], in_=cs[:], func=AF.Sigmoid), reads=[cs], writes=[sg])
        P.op(P.dve, lambda e: e.tensor_tensor(out=cs[:], in0=cs[:], in1=sg[:], op=ALU.mult), reads=[cs, sg], writes=[cs])
        for l in range(DEPTH):
            P.dma(P.sp, bm[:], bmod[l], out_tile=bm)
            for n in range(12):
                w = wr.next()
                P.dma(P.sp, w[:], wmod[l][:, n * 512:(n + 1) * 512].rearrange("(k p) n -> p k n", p=128), out_tile=w)
                ps = self.psb[n % 2]
                for k in range(8):
                    P.op(P.pe, lambda e, k=k: e.matmul(ps[0:2, :], lhsT=cs[:, k, :], rhs=w[:, k, :], start=(k == 0), stop=(k == 7)),
                         reads=[cs, w], writes=[ps], mark=(k == 7))
                P.op(P.dve, lambda e: e.tensor_tensor(out=mr[:, n * 512:(n + 1) * 512], in0=ps[0:2, :], in1=bm[:, n * 512:(n + 1) * 512], op=ALU.add),
                     reads=[ps, bm], writes=[mr])
            P.dma(P.sp, modrow[l], mr[:], out_tile=modrow, in_tile=mr)

    def load_mod_tiles(self, tiles, idxs, r, gain_row):
        P = self.P
        l = self.layer
        for t, i in zip(tiles, idxs):
            P.dma(P.sp, t[:], self.modrow[l][r:r + 1, i * D:(i + 1) * D].partition_broadcast(128), out_tile=t, in_tile=self.modrow)
        if gain_row is not None:
            g = self.gtile
            P.dma(P.sp, g[:], gain_row.partition_broadcast(128), out_tile=g)
            t = tiles[0]
            P.op(P.dve, lambda e: e.scalar_tensor_tensor(out=t[:], in0=t[:], scalar=1.0, in1=g[:], op0=ALU.add, op1=ALU.mult),
                 reads=[t, g], writes=[t])

    def norm_transpose(self, xt, s, G, SH, xnT, col0, stat, junk, xn32, xnb):
        P = self.P
        P.op(P.act, lambda e: e.activation(out=junk[:], in_=xt[:, s, :], func=AF.Square, accum_out=stat[:, 0:1]), reads=[xt], writes=[junk, stat])
        P.op(P.act, lambda e: e.activation(out=stat[:, 1:2], in_=stat[:, 0:1], func=AF.Sqrt, scale=1.0 / D, bias=self.epsb[:, 1:2]), reads=[stat, self.epsb], writes=[stat])
        P.op(P.dve, lambda e: e.reciprocal(out=stat[:, 2:3], in_=stat[:, 1:2]), reads=[stat], writes=[stat])
        P.op(P.dve, lambda e: e.scalar_tensor_tensor(out=xn32[:], in0=xt[:, s, :], scalar=stat[:, 2:3], in1=G[:], op0=ALU.mult, op1=ALU.mult),
             reads=[xt, stat, G], writes=[xn32])
        P.op(P.pool, lambda e: e.tensor_tensor(out=xnb[:], in0=xn32[:], in1=SH[:], op=ALU.add), reads=[xn32, SH], writes=[xnb])
        tp = self.pstp
        for k in range(8):
            P.op(P.pe, lambda e, k=k: e.transpose(out=tp[:, k * 128:(k + 1) * 128], in_=xnb[:, k * 128:(k + 1) * 128], identity=self.ident[:]),
                 reads=[xnb, self.ident], writes=[tp], mark=(k == 7))
        P.op(P.act, lambda e: e.activation(out=xnT[:, :, col0:col0 + 128], in_=tp[:].rearrange("p (k t) -> p k t", k=8), func=AF.Copy),
             reads=[tp], writes=[xnT])

    def load_weight_bf16(self, wt, src, ncols, kchunks=8, split=1):
        P = self.P
        if getattr(self, "_wstage_pst", None) is not self.pst:
            self._wstage_n = getattr(self, "_wstage_n", 0) + 1
            self._wstage = P.ring("wstg%d_" % self._wstage_n, 2, [128, 1024], F32, self.pst)
            self._wstage_pst = self.pst
            self._wcast_i = 0
        W = 1024
        for k in range(kchunks):
            c0 = 0
            while c0 < ncols:
                w = min(W, ncols - c0)
                stg = self._wstage.next()
                P.dma(P.sp, stg[:, 0:w], src[k * 128:(k + 1) * 128, c0:c0 + w], out_tile=stg)
                which = self._wcast_i % 3
                self._wcast_i += 1
                if which == 0:
                    P.op(P.pool, lambda e, stg=stg, k=k, c0=c0, w=w: e.tensor_copy(out=wt[:, k, c0:c0 + w], in_=stg[:, 0:w]), reads=[stg], writes=[wt])
                elif which == 1:
                    P.op(P.dve, lambda e, stg=stg, k=k, c0=c0, w=w: e.tensor_copy(out=wt[:, k, c0:c0 + w], in_=stg[:, 0:w]), reads=[stg], writes=[wt])
                else:
                    P.op(P.act, lambda e, stg=stg, k=k, c0=c0, w=w: e.activation(out=wt[:, k, c0:c0 + w], in_=stg[:, 0:w], func=AF.Copy), reads=[stg], writes=[wt])
                c0 += w

    def phase_A(self):
        P, pst = self.P, self.pst
        l = self.layer = 0
        ext_in = "M" not in self.phases
        self.modrow = self.dio("modrow", [DEPTH, 2, 6 * D], F32, ext_in, False)
        xres = self.dio("xres", [NT, D], F32, True, False)
        w_fm = self.din("w_in_fm", [D, NFM])
        w_tm = self.din("w_in_tm", [D, NTM])
        g1 = self.din("norm1_g", [1, D])
        wg = self.din("gate_w", [32, 512])
        bg = self.din("gate_b", [1, 512])
        cosT = self.din("cosT", [128, NT])
        sinT = self.din("sinT", [128, NT])
        c_matf = self.din("c_matf", [128, 128])
        c_matb = self.din("c_matb", [128, 128])
        c_misc = self.din("c_misc", [128, 8])
        out_ext = "B" not in self.phases
        fmT = self.dio("fmT", [NFMO, 128, NT], BF16, False, out_ext)
        tmS = self.dio("tmS", [NT, NTM], BF16, False, out_ext)
        spS = self.dio("spS", [NT, 512], F32, False, out_ext)
        ckv = self.dio("ckv", [128, NCH, 256], F32, False, out_ext)
        cdd = self.dio("cdd", [128, NCH, 4], F32, False, out_ext)
        summ = self.dio("summ", [128, 2, 260], F32, False, out_ext)

        wfm = P.sbuf("a_wfm", [128, 8, NFM], BF16, pst)
        wtm = P.sbuf("a_wtm", [128, 8, NTM], BF16, pst)
        wgb = P.sbuf("a_wg", [32, 512], BF16, pst)
        bgb = P.sbuf("a_bg", [1, 512], BF16, pst)
        ones1 = P.sbuf("a_ones", [1, 128], BF16, pst)
        matf = P.sbuf("a_matf", [128, 128], F32, pst)
        matb = P.sbuf("a_matb", [128, 128], F32, pst)
        misc = P.sbuf("a_misc", [128, 8], F32, pst)
        self.epsb = misc
        G1 = P.sbuf("a_G1", [128, D], F32, pst)
        SH1 = P.sbuf("a_SH1", [128, D], F32, pst)
        self.gtile = P.sbuf("a_gt", [128, D], F32, pst)
        xring = P.ring("a_x", 2, [128, 4, D], F32, pst)
        xnTring = P.ring("a_xnT", 2, [128, 8, 512], BF16, pst)
        stat = P.ring("a_stat", 2, [128, 4], F32, pst)
        xn32 = P.ring("a_xn32", 2, [128, D], F32, pst)
        xnb = P.ring("a_xnb", 2, [128, D], BF16, pst)
        stg = P.ring("a_stg", 3, [128, 512], BF16, pst)
        tmp32 = P.ring("a_t32", 3, [128, 512], F32, pst)
        ropec = P.ring("a_rc", 2, [128, 512], F32, pst)
        ropes = P.ring("a_rs", 2, [128, 512], F32, pst)
        lrT = P.ring("a_lrT", 2, [32, 512], BF16, pst)
        tmst = P.ring("a_tmst", 2, [128, NTM], BF16, pst)
        spt = P.ring("a_sp", 2, [128, 512], F32, pst)
        et = P.ring("a_e", 2, [128, 512], F32, pst)
        kh = P.ring("a_kh", 2, [128, 512], BF16, pst)
        kvr = P.ring("a_kvt", 2, [128, 256], F32, pst)
        ddall = P.sbuf("a_ddall", [128, NCH, 4], F32, pst)
        sm = P.sbuf("a_sm", [128, 2, 260], F32, pst)

        P.dma(P.sp, misc[:], c_misc[:], out_tile=misc)
        P.dma(P.sp, matf[:], c_matf[:], out_tile=matf)
        P.dma(P.sp, matb[:], c_matb[:], out_tile=matb)
        P.dma(P.pool, wgb[:], wg[:], out_tile=wgb)
        P.dma(P.pool, bgb[:], bg[:], out_tile=bgb)
        P.op(P.dve, lambda e: e.memset(ones1[:], 1.0), writes=[ones1])
        self.load_mod_tiles([G1, SH1], [1, 0], 0, g1[0:1, :])
        self.load_weight_bf16(wfm, w_fm, NFM)
        self.load_weight_bf16(wtm, w_tm, NTM)

        ps_fm = Ring([self.psb[0], self.psb[1], self.psb[2]])
        ps_ta, ps_tb, ps_z, ps_r = self.psb[3], self.psb[4], self.psb[5], self.psb[6]
        plan = []
        plan += [(0, 128, "pm_a", None), (2 * 128, 128, "pm_b", C_U + 0), (1 * 128, 128, "pm_a", None), (3 * 128, 128, "pm_b", C_U + 1)]
        plan += [(4 * 128, 128, "copy", C_B + 0), (5 * 128, 128, "copy", C_B + 1)]
        for i in range(4):
            plan += [((6 + i) * 128, 128, "rope_a", None), ((10 + i) * 128, 128, "rope_b", C_Q + i)]
        for i in range(2):
            plan += [((14 + i) * 128, 128, "rope_a", None), ((16 + i) * 128, 128, "rope_b", C_K + i)]
        for i in range(2):
            plan += [((18 + i) * 128, 128, "copy", C_GQ + i)]
        for i in range(2):
            plan += [((20 + i) * 128, 128, "copy", C_GK + i)]
        for i in range(2):
            plan += [((22 + i) * 128, 128, "copy", C_GG + i)]
        plan += [(24 * 128, 32, "lr", None)]

        mts = macro_tiles()
        loaded = {}

        def load_mt(i):
            (t0, T, is_ctx) = mts[i]
            xt = xring.next()
            P.dma(P.sp, xt[:, 0:T // 128, :], xres[t0:t0 + T, :].rearrange("(s p) f -> p s f", p=128), out_tile=xt, in_tile=xres)
            rc, rs = ropec.next(), ropes.next()
            P.dma(P.sp, rc[:, 0:T], cosT[:, t0:t0 + T], out_tile=rc)
            P.dma(P.sp, rs[:, 0:T], sinT[:, t0:t0 + T], out_tile=rs)
            loaded[i] = (xt, rc, rs)

        load_mt(0)
        for mi, (t0, T, is_ctx) in enumerate(mts):
            nt = T // 128
            if mi + 1 < len(mts):
                load_mt(mi + 1)
            if is_ctx:
                self.load_mod_tiles([G1, SH1], [1, 0], 1, g1[0:1, :])
            xt, rc, rs = loaded.pop(mi)
            xnT = xnTring.next()
            for s in range(nt):
                x32_ = xn32.next()
                self.norm_transpose(xt, s, G1, SH1, xnT, s * 128, stat.next(), x32_, x32_, xnb.next())
            lr = lrT.next()
            hold = None
            for (c0, M, kind, oc) in plan:
                ps = ps_fm.next()
                for k in range(8):
                    P.op(P.pe, lambda e, k=k, ps=ps: e.matmul(ps[0:M, 0:T], lhsT=wfm[:, k, c0:c0 + M], rhs=xnT[:, k, 0:T], start=(k == 0), stop=(k == 7)),
                         reads=[wfm, xnT], writes=[ps], mark=(k == 7))
                if kind == "copy":
                    sg_ = stg.next()
                    P.op(P.act, lambda e, ps=ps, sg_=sg_: e.activation(out=sg_[:, 0:T], in_=ps[:, 0:T], func=AF.Copy), reads=[ps], writes=[sg_])
                    P.dma(P.sp, fmT[oc][:, t0:t0 + T], sg_[:, 0:T], out_tile=fmT, in_tile=sg_)
                elif kind == "lr":
                    P.op(P.act, lambda e, ps=ps: e.activation(out=lr[:, 0:T], in_=ps[0:32, 0:T], func=AF.Copy), reads=[ps], writes=[lr])
                elif kind == "pm_a":
                    hold = tmp32.next()
                    P.op(P.act, lambda e, ps=ps, h=hold: e.activation(out=h[:, 0:T], in_=ps[:, 0:T], func=AF.Copy), reads=[ps], writes=[hold])
                elif kind == "pm_b":
                    sg_ = stg.next()
                    P.op(P.dve, lambda e, ps=ps, sg_=sg_, h=hold: e.tensor_tensor(out=sg_[:, 0:T], in0=ps[:, 0:T], in1=h[:, 0:T], op=ALU.mult),
                         reads=[ps, hold], writes=[sg_])
                    P.dma(P.sp, fmT[oc][:, t0:t0 + T], sg_[:, 0:T], out_tile=fmT, in_tile=sg_)
                elif kind == "rope_a":
                    hold = tmp32.next()
                    P.op(P.dve, lambda e, ps=ps, h=hold: e.tensor_tensor(out=h[:, 0:T], in0=ps[:, 0:T], in1=rc[:, 0:T], op=ALU.mult),
                         reads=[ps, rc], writes=[hold])
                elif kind == "rope_b":
                    h2 = tmp32.next()
                    sg_ = stg.next()
                    P.op(P.dve, lambda e, ps=ps, h2=h2: e.tensor_tensor(out=h2[:, 0:T], in0=ps[:, 0:T], in1=rs[:, 0:T], op=ALU.mult),
                         reads=[ps, rs], writes=[h2])
                    P.op(P.pool, lambda e, h=hold, h2=h2, sg_=sg_: e.tensor_tensor(out=sg_[:, 0:T], in0=h[:, 0:T], in1=h2[:, 0:T], op=ALU.add),
                         reads=[hold, h2], writes=[sg_])
                    P.dma(P.sp, fmT[oc][:, t0:t0 + T], sg_[:, 0:T], out_tile=fmT, in_tile=sg_)
            for s in range(nt):
                n = (t0 // 128) + s
                tsl = slice(s * 128, (s + 1) * 128)
                for k in range(8):
                    P.op(P.pe, lambda e, k=k: e.matmul(ps_ta[:, 0:512], lhsT=xnT[:, k, tsl], rhs=wtm[:, k, 0:512], start=(k == 0), stop=(k == 7)),
                         reads=[xnT, wtm], writes=[ps_ta], mark=(k == 7))
                pstb_a = ps_tb
                for k in range(8):
                    P.op(P.pe, lambda e, k=k: e.matmul(ps_tb[:, 0:256], lhsT=xnT[:, k, tsl], rhs=wtm[:, k, 512:768], start=(k == 0), stop=(k == 7)),
                         reads=[xnT, wtm], writes=[ps_tb], mark=(k == 7))
                tm = tmst.next()
                P.op(P.act, lambda e, tm=tm: e.activation(out=tm[:, 0:512], in_=ps_ta[:, 0:512], func=AF.Copy), reads=[ps_ta], writes=[tm])
                P.op(P.dve, lambda e, tm=tm: e.tensor_copy(out=tm[:, 512:768], in_=ps_tb[:, 0:256]), reads=[ps_tb], writes=[tm])
                P.dma(P.sp, tmS[n * 128:(n + 1) * 128, :], tm[:], out_tile=tmS, in_tile=tm)
                P.op(P.pe, lambda e: e.matmul(ps_z[:, :], lhsT=lr[:, tsl], rhs=wgb[:, :], start=True, stop=False), reads=[lr, wgb], writes=[ps_z], mark=False)
                P.op(P.pe, lambda e: e.matmul(ps_z[:, :], lhsT=ones1[:, :], rhs=bgb[:, :], start=False, stop=True), reads=[ones1, bgb], writes=[ps_z])
                e1 = et.next()
                sp_ = spt.next()
                P.op(P.act, lambda e, e1=e1: e.activation(out=e1[:], in_=ps_z[:], func=AF.Exp, scale=-1.0), reads=[ps_z], writes=[e1])
                P.op(P.act, lambda e, e1=e1, sp_=sp_: e.activation(out=sp_[:], in_=e1[:], func=AF.Ln, bias=misc[:, 2:3]), reads=[e1, misc], writes=[sp_])
                P.dma(P.sp, spS[n * 128:(n + 1) * 128, :], sp_[:], out_tile=spS, in_tile=sp_)
                P.op(P.pe, lambda e, sp_=sp_: e.matmul(ps_r[:, 0:256], lhsT=matf[:], rhs=sp_[:, 0:256], start=True, stop=True), reads=[matf, sp_], writes=[ps_r], mark=False)
                P.op(P.pe, lambda e, sp_=sp_: e.matmul(ps_r[:, 256:512], lhsT=matb[:], rhs=sp_[:, 256:512], start=True, stop=True), reads=[matb, sp_], writes=[ps_r])
                e2 = et.next()
                P.op(P.act, lambda e, e2=e2: e.activation(out=e2[:], in_=ps_r[:], func=AF.Exp), reads=[ps_r], writes=[e2])
                kh_ = kh.next()
                P.op(P.dve, lambda e, e2=e2, kh_=kh_: e.tensor_tensor(out=kh_[:, 0:256], in0=ps_ta[:, 0:256], in1=e2[:, 0:256], op=ALU.mult), reads=[ps_ta, e2], writes=[kh_])
                P.op(P.dve, lambda e, e2=e2, kh_=kh_: e.tensor_tensor(out=kh_[:, 256:512], in0=ps_ta[:, 0:256], in1=e2[:, 256:512], op=ALU.mult), reads=[ps_ta, e2], writes=[kh_])
                for d in range(2):
                    for hp in range(2):
                        j = d * 2 + hp
                        P.op(P.pe, lambda e, sp_=sp_, d=d, hp=hp, j=j: e.matmul(ps_z[:, j:j + 1], lhsT=sp_[:, d * 256 + hp * 128:d * 256 + (hp + 1) * 128], rhs=misc[:, 0:1], start=True, stop=True),
                             reads=[sp_, misc], writes=[ps_z], mark=(j == 3))
                P.op(P.act, lambda e, n=n: e.activation(out=ddall[:, n, :], in_=ps_z[:, 0:4], func=AF.Exp), reads=[ps_z], writes=[ddall])
                for d in range(2):
                    for h in range(4):
                        po = (h % 2) * 64
                        co = 256 + d * 128 + (h // 2) * 64
                        P.op(P.pe, lambda e, d=d, h=h, po=po, co=co, kh_=kh_, tm=tm: e.matmul(ps_tb[po:po + 64, co:co + 64], lhsT=kh_[:, d * 256 + h * 64:d * 256 + (h + 1) * 64],
                                                                                   rhs=tm[:, 256 + h * 64:256 + (h + 1) * 64], start=True, stop=True),
                             reads=[kh_, tm], writes=[ps_tb], mark=(d == 1 and h == 3))
                kvt = kvr.next()
                P.op(P.act, lambda e, kvt=kvt: e.activation(out=kvt[:], in_=ps_tb[:, 256:512], func=AF.Copy), reads=[ps_tb], writes=[kvt])
                P.dma(P.sp, ckv[:, n, :], kvt[:], out_tile=ckv, in_tile=kvt)
                seg = 1 if is_ctx else 0
                first = (n == 0) or (n == NLAT // 128)
                for d in range(2):
                    for hp in range(2):
                        j = d * 2 + hp
                        sl = slice(d * 128 + hp * 64, d * 128 + (hp + 1) * 64)
                        dj = slice(256 + j, 257 + j)
                        if first:
                            P.op(P.dve, lambda e, sl=sl, kvt=kvt: e.tensor_copy(out=sm[:, seg, sl], in_=kvt[:, sl]), reads=[kvt], writes=[sm])
                            P.op(P.dve, lambda e, dj=dj, j=j, n=n: e.tensor_copy(out=sm[:, seg, dj], in_=ddall[:, n, j:j + 1]), reads=[ddall], writes=[sm])
                        else:
                            if d == 0:
                                P.op(P.dve, lambda e, sl=sl, kvt=kvt, j=j, n=n: e.scalar_tensor_tensor(out=sm[:, seg, sl], in0=sm[:, seg, sl], scalar=ddall[:, n, j:j + 1], in1=kvt[:, sl],
                                                                                                op0=ALU.mult, op1=ALU.add), reads=[sm, ddall, kvt], writes=[sm])
                            else:
                                P.op(P.dve, lambda e, sl=sl, kvt=kvt, dj=dj: e.scalar_tensor_tensor(out=sm[:, seg, sl], in0=kvt[:, sl], scalar=sm[:, seg, dj], in1=sm[:, seg, sl],
                                                                                              op0=ALU.mult, op1=ALU.add), reads=[sm, kvt], writes=[sm])
                            P.op(P.dve, lambda e, dj=dj, j=j, n=n: e.tensor_tensor(out=sm[:, seg, dj], in0=sm[:, seg, dj], in1=ddall[:, n, j:j + 1], op=ALU.mult),
                                 reads=[sm, ddall], writes=[sm])
        P.dma(P.sp, cdd[:], ddall[:], out_tile=cdd, in_tile=ddall)
        P.dma(P.sp, summ[:], sm[:], out_tile=summ, in_tile=sm)

    def chain_summary(self, sm, kvall, ddall):
        P = self.P
        for seg, (lo, hi) in enumerate([(0, NLAT // 128), (NLAT // 128, NCH)]):
            for d in range(2):
                order = list(range(lo, hi)) if d == 0 else list(range(hi - 1, lo - 1, -1))
                for i, n in enumerate(order):
                    for hp in range(2):
                        j = d * 2 + hp
                        sl = slice(d * 128 + hp * 64, d * 128 + (hp + 1) * 64)
                        if i == 0:
                            P.op(P.dve, lambda e, n=n, sl=sl: e.tensor_copy(out=sm[:, seg, sl], in_=kvall[:, n, sl]), reads=[kvall], writes=[sm])
                            P.op(P.dve, lambda e, n=n, j=j: e.tensor_copy(out=sm[:, seg, 256 + j:257 + j], in_=ddall[:, n, j:j + 1]), reads=[ddall], writes=[sm])
                        else:
                            P.op(P.dve, lambda e, n=n, sl=sl, j=j: e.scalar_tensor_tensor(out=sm[:, seg, sl], in0=sm[:, seg, sl], scalar=ddall[:, n, j:j + 1], in1=kvall[:, n, sl],
                                                                                    op0=ALU.mult, op1=ALU.add), reads=[sm, ddall, kvall], writes=[sm])
                            P.op(P.dve, lambda e, n=n, j=j: e.tensor_tensor(out=sm[:, seg, 256 + j:257 + j], in0=sm[:, seg, 256 + j:257 + j], in1=ddall[:, n, j:j + 1], op=ALU.mult),
                                 reads=[sm, ddall], writes=[sm])


def _rope_partner():
    perm = np.zeros(64, np.int64)
    sign = np.zeros(64, np.float32)
    for d in range(64):
        e = d % 32
        if e < 16:
            perm[d] = d + 16
            sign[d] = -1.0
        else:
            perm[d] = d - 16
            sign[d] = 1.0
    return perm, sign


def fm_cols():
    perm, _ = _rope_partner()
    cols = []
    cols += list(range(0, 256))
    cols += list(range(512, 768))
    cols += list(range(256, 512))
    q0 = 768
    cols += list(range(q0, q0 + 512))
    for h in range(8):
        cols += [q0 + h * 64 + int(perm[d]) for d in range(64)]
    k0 = 1280
    for g in range(2):
        cols += list(range(k0 + g * 64, k0 + (g + 1) * 64)) * 2
    for g in range(2):
        cols += [k0 + g * 64 + int(perm[d]) for d in range(64)] * 2
    cols += list(range(1536, 1792))
    cols += list(range(1792, 2048))
    cols += list(range(2304, 2560))
    cols += list(range(2560, 2592))
    assert len(cols) == NFM
    return np.array(cols)


def tm_cols():
    cols = list(range(1792, 2048)) + list(range(2048, 2304))
    v0 = 1408
    for g in range(2):
        cols += list(range(v0 + g * 64, v0 + (g + 1) * 64)) * 2
    assert len(cols) == NTM
    return np.array(cols)


def rope_tables(seg):
    _, sign = _rope_partner()
    t = np.arange(seg * NLAT, (seg + 1) * NLAT)
    row = (t // 64).astype(np.float32)
    col = (t % 64).astype(np.float32)
    inv = (np.float32(10000.0) ** (-np.arange(16, dtype=np.float32) / np.float32(16))).astype(np.float32)
    cosT = np.ones((64, NT), np.float32)
    sinT = np.zeros((64, NT), np.float32)
    for d in range(64):
        pos = row if d < 32 else col
        ang = (pos * inv[d % 16]).astype(np.float32)
        cosT[d, :NLAT] = np.cos(ang)
        sinT[d, :NLAT] = np.sin(ang) * sign[d]
    return np.concatenate([cosT, cosT], 0), np.concatenate([sinT, sinT], 0)


def const_inputs():
    s = np.arange(128)
    c = {}
    c["c_ident"] = np.eye(128, dtype=np.float32)
    g = np.float32(-1.0 / 16.0)
    c["c_matf"] = np.where(s[:, None] > s[None, :], g, 0).astype(np.float32)
    c["c_matb"] = np.where(s[:, None] < s[None, :], g, 0).astype(np.float32)
    misc = np.zeros((128, 8), np.float32)
    misc[:, 0] = g
    misc[:, 1] = EPS
    misc[:, 2] = 1.0
    misc[:, 3] = np.log(np.float32(0.125))
    c["c_misc"] = misc
    return c


def layer_inputs_A(inp, l):
    d = {}
    d["w_in_fm"] = np.ascontiguousarray(inp["w_in"][l][:, fm_cols()])
    d["w_in_tm"] = np.ascontiguousarray(inp["w_in"][l][:, tm_cols()])
    d["norm1_g"] = inp["norm1_g"][l][None, :]
    wg = np.zeros((32, 512), np.float32)
    wg[0:16, 0:256] = inp["gla_gate_w"][l][0]
    wg[16:32, 256:512] = inp["gla_gate_w"][l][1]
    d["gate_w"] = wg
    d["gate_b"] = np.concatenate([inp["gla_gate_b"][l][0], inp["gla_gate_b"][l][1]])[None, :]
    return d


def core_inputs_M(inp, b):
    cT = np.stack([inp["c"][b], inp["c_ctx"]], -1).reshape(8, 128, 2).transpose(1, 0, 2)
    return {"cT": np.ascontiguousarray(cT), "w_mod": inp["w_mod"],
            "b_mod2": np.ascontiguousarray(np.repeat(inp["b_mod"][:, None, :], 2, axis=1))}


def _phase_B(self):
    P, pst = self.P, self.pst
    ext_in = "A" not in self.phases
    fmT = self.dio("fmT", [NFMO, 128, NT], BF16, ext_in, False)
    tmS = self.dio("tmS", [NT, NTM], BF16, ext_in, False)
    spS = self.dio("spS", [NT, 512], F32, ext_in, False)
    ckv = self.dio("ckv", [128, NCH, 256], F32, ext_in, False)
    cdd = self.dio("cdd", [128, NCH, 4], F32, ext_in, False)
    summ = self.dio("summ", [128, 2, 260], F32, ext_in, False)
    summ_all = self.din("summ_all", [128, 4, 260])
    chainmask = self.din("chainmask", [128, 8])
    kT_halo = self.din("kT_halo", [128, 2, 256], BF16)
    v_halo = self.din("v_halo", [128, 2, 256], BF16)
    u_halo = self.din("u_halo", [128, 2, 2], BF16)
    amask = self.din("amask", [128, 4, 512])
    gmask = self.din("gmask", [128, 2, 512])
    c_m2 = self.din("c_m2", [128, 2, 128])
    c_bones = self.din("c_bones", [128, 128])
    c_misc = self.din("c_misc", [128, 8])
    cw = self.din("conv_cw", [128, 2, 3])
    sink = self.din("sink_row", [1, 8])
    ngc = self.din("gla_ng", [128, 1])
    catT = self.dio("catT", [8, 128, NT], BF16, False, "O" not in self.phases)

    misc = P.sbuf("b_misc", [128, 8], F32, pst)
    m2 = P.sbuf("b_m2", [128, 2, 128], F32, pst)
    bones = P.sbuf("b_bones", [128, 128], F32, pst)
    gm = P.sbuf("b_gm", [128, 2, 512], BF16, pst)
    am = P.sbuf("b_am", [128, 4, 512], BF16, pst)
    cwt = P.sbuf("b_cw", [128, 2, 3], F32, pst)
    ng = P.sbuf("b_ng", [128, 1], F32, pst)
    skr = P.sbuf("b_skr", [128, 8], F32, pst)
    onesf = P.sbuf("b_onesf", [128, 128], F32, pst)
    onesb = P.sbuf("b_onesb", [128, 128], BF16, pst)
    sexp = P.sbuf("b_sexp", [128, 2, 512], F32, pst)
    sa = P.sbuf("b_sa", [128, 4, 260], F32, pst)
    so = P.sbuf("b_so", [128, 2, 260], F32, pst)
    cm = P.sbuf("b_cm", [128, 8], F32, pst)
    kva = P.sbuf("b_kva", [128, NCH, 256], F32, pst)
    dda = P.sbuf("b_dda", [128, NCH, 4], F32, pst)
    Sall = P.sbuf("b_Sall", [128, NCH, 256], BF16, pst)
    Sf = P.sbuf("b_Sf", [128, 128], F32, pst)
    Sb = P.sbuf("b_Sb", [128, 128], F32, pst)
    T1f = P.sbuf("b_T1f", [128, 128], F32, pst)
    T1b = P.sbuf("b_T1b", [128, 128], F32, pst)

    P.dma(P.sp, misc[:], c_misc[:], out_tile=misc)
    P.dma(P.sp, m2[:], c_m2[:], out_tile=m2)
    P.dma(P.sp, bones[:], c_bones[:], out_tile=bones)
    P.dma(P.pool, gm[:], gmask[:], out_tile=gm)
    P.dma(P.pool, am[:], amask[:], out_tile=am)
    P.dma(P.sp, cwt[:], cw[:], out_tile=cwt)
    P.dma(P.sp, ng[:], ngc[:], out_tile=ng)
    P.dma(P.sp, skr[:], sink[0:1, :].partition_broadcast(128), out_tile=skr)
    P.dma(P.sp, sa[:], summ_all[:], out_tile=sa)
    P.dma(P.sp, so[:], summ[:], out_tile=so, in_tile=summ)
    P.dma(P.sp, cm[:], chainmask[:], out_tile=cm)
    P.dma(P.sp, kva[:], ckv[:], out_tile=kva, in_tile=ckv)
    P.dma(P.sp, dda[:], cdd[:], out_tile=dda, in_tile=cdd)
    P.op(P.dve, lambda e: e.memset(onesf[:], 1.0), writes=[onesf])
    P.op(P.dve, lambda e: e.memset(onesb[:], 1.0), writes=[onesb])
    P.op(P.act, lambda e: e.activation(out=skr[:], in_=skr[:], func=AF.Exp), reads=[skr], writes=[skr])
    for h in range(8):
        P.op(P.dve, lambda e, h=h: e.tensor_scalar(out=sexp[:, h // 4, (h % 4) * 128:(h % 4 + 1) * 128], in0=onesf[:], scalar1=skr[:, h:h + 1], scalar2=None, op0=ALU.mult),
             reads=[onesf, skr], writes=[sexp])

    with self.nc.named_scope("B0_chain"):
        for (E, d, S, T1) in [(P.dve, 0, Sf, T1f), (P.dve, 1, Sb, T1b)]:
            P.op(E, lambda e, d=d, S=S: e.tensor_copy(out=S[:], in_=so[:, 1, d * 128:(d + 1) * 128]), reads=[so], writes=[S])
            order = [0, 1, 2, 3] if d == 0 else [3, 2, 1, 0]
            for i in order:
                for hp in range(2):
                    sl = slice(hp * 64, (hp + 1) * 64)
                    P.op(E, lambda e, i=i, hp=hp, sl=sl, d=d, S=S, T1=T1: e.scalar_tensor_tensor(out=T1[:, sl], in0=S[:, sl], scalar=sa[:, i, 256 + d * 2 + hp:257 + d * 2 + hp],
                                                                                           in1=sa[:, i, d * 128 + hp * 64:d * 128 + (hp + 1) * 64], op0=ALU.mult, op1=ALU.add),
                         reads=[S, sa], writes=[T1])
                P.op(E, lambda e, S=S, T1=T1: e.tensor_tensor(out=T1[:], in0=T1[:], in1=S[:], op=ALU.subtract), reads=[T1, S], writes=[T1])
                P.op(E, lambda e, i=i, d=d, S=S, T1=T1: e.scalar_tensor_tensor(out=S[:], in0=T1[:], scalar=cm[:, d * 4 + i:d * 4 + i + 1], in1=S[:], op0=ALU.mult, op1=ALU.add),
                     reads=[T1, cm, S], writes=[S])
            order = list(range(32)) if d == 0 else list(range(31, -1, -1))
            for n in order:
                P.op(E, lambda e, n=n, d=d, S=S: e.tensor_copy(out=Sall[:, n, d * 128:(d + 1) * 128], in_=S[:]), reads=[S], writes=[Sall])
                for hp in range(2):
                    sl = slice(hp * 64, (hp + 1) * 64)
                    P.op(E, lambda e, n=n, hp=hp, sl=sl, d=d, S=S: e.scalar_tensor_tensor(out=S[:, sl], in0=S[:, sl], scalar=dda[:, n, d * 2 + hp:d * 2 + hp + 1],
                                                                                    in1=kva[:, n, d * 128 + hp * 64:d * 128 + (hp + 1) * 64], op0=ALU.mult, op1=ALU.add),
                         reads=[S, dda, kva], writes=[S])
            zc, oc_, src = (32, 33, 32) if d == 0 else (33, 32, 33)
            P.op(E, lambda e, zc=zc, d=d: e.memset(Sall[:, zc, d * 128:(d + 1) * 128], 0.0), writes=[Sall])
            P.op(E, lambda e, oc_=oc_, src=src, d=d: e.tensor_copy(out=Sall[:, oc_, d * 128:(d + 1) * 128], in_=kva[:, src, d * 128:(d + 1) * 128]), reads=[kva], writes=[Sall])

    with self.nc.named_scope("B1_gla"):
        gqr = P.ring("b_gq", 2, [128, 2, 128], BF16, pst)
        gkr = P.ring("b_gk", 2, [128, 2, 128], BF16, pst)
        ggr = P.ring("b_gg", 2, [128, 2, 128], BF16, pst)
        tmr = P.ring("b_tm", 2, [128, 256], BF16, pst)
        spr = P.ring("b_sp", 2, [128, 512], F32, pst)
        eqr = P.ring("b_eq", 2, [128, 4, 128], F32, pst)
        ekr = P.ring("b_ek", 2, [128, 4, 128], F32, pst)
        qtr = P.ring("b_qt", 2, [128, 4, 128], BF16, pst)
        ktr = P.ring("b_kt", 2, [128, 4, 128], BF16, pst)
        amr = P.ring("b_amr", 2, [128, 2, 512], BF16, pst)
        o32r = P.ring("b_o32", 2, [128, 256], F32, pst)
        sqr = P.ring("b_sq", 2, [128, 256], F32, pst)
        rsr = P.ring("b_rs", 2, [128, 256], F32, pst)
        sgr = P.ring("b_sg", 2, [128, 256], F32, pst)
        ygr = P.ring("b_yg", 2, [128, 2, 128], BF16, pst)
        ps_cb, ps_af, ps_ab, ps_o, ps_m = self.psb[0], self.psb[1], self.psb[2], self.psb[3], self.psb[4]
        for n in range(NCH):
            tk = slice(n * 128, (n + 1) * 128)
            gq, gk, gg, tm, sp_ = gqr.next(), gkr.next(), ggr.next(), tmr.next(), spr.next()
            P.dma(P.sp, gq[:], fmT[C_GQ:C_GQ + 2][:, :, tk].rearrange("c p t -> p c t"), out_tile=gq, in_tile=fmT)
            P.dma(P.sp, gk[:], fmT[C_GK:C_GK + 2][:, :, tk].rearrange("c p t -> p c t"), out_tile=gk, in_tile=fmT)
            P.dma(P.sp, gg[:], fmT[C_GG:C_GG + 2][:, :, tk].rearrange("c p t -> p c t"), out_tile=gg, in_tile=fmT)
            P.dma(P.sp, tm[:], tmS[tk, 256:512], out_tile=tm, in_tile=tmS)
            P.dma(P.sp, sp_[:], spS[tk, :], out_tile=sp_, in_tile=spS)
            for d in range(2):
                for hp in range(2):
                    j = d * 2 + hp
                    P.op(P.pe, lambda e, d=d, hp=hp, j=j, sp_=sp_: e.matmul(ps_cb[:, j * 128:(j + 1) * 128], lhsT=sp_[:, d * 256 + hp * 128:d * 256 + (hp + 1) * 128], rhs=m2[:, d, :], start=True, stop=True),
                         reads=[sp_, m2], writes=[ps_cb], mark=(j == 3))
            eq, ek, qt, kt = eqr.next(), ekr.next(), qtr.next(), ktr.next()
            P.op(P.act, lambda e, eq=eq: e.activation(out=eq[:].rearrange("p a t -> p (a t)"), in_=ps_cb[:], func=AF.Exp, bias=misc[:, 3:4]), reads=[ps_cb, misc], writes=[eq])
            P.op(P.act, lambda e, ek=ek: e.activation(out=ek[:].rearrange("p a t -> p (a t)"), in_=ps_cb[:], func=AF.Exp, scale=-1.0), reads=[ps_cb], writes=[ek])
            for d in range(2):
                P.op(P.dve, lambda e, d=d, qt=qt, gq=gq, eq=eq: e.tensor_tensor(out=qt[:, d * 2:d * 2 + 2, :], in0=gq[:], in1=eq[:, d * 2:d * 2 + 2, :], op=ALU.mult), reads=[gq, eq], writes=[qt])
                P.op(P.pool, lambda e, d=d, kt=kt, gk=gk, ek=ek: e.tensor_tensor(out=kt[:, d * 2:d * 2 + 2, :], in0=gk[:], in1=ek[:, d * 2:d * 2 + 2, :], op=ALU.mult), reads=[gk, ek], writes=[kt])
            am_ = amr.next()
            for d in range(2):
                psa = ps_af if d == 0 else ps_ab
                for h in range(4):
                    pr = slice((h % 2) * 64, (h % 2 + 1) * 64)
                    P.op(P.pe, lambda e, d=d, h=h, pr=pr, psa=psa, kt=kt, qt=qt: e.matmul(psa[:, h * 128:(h + 1) * 128], lhsT=kt[pr, d * 2 + h // 2, :], rhs=qt[pr, d * 2 + h // 2, :], start=True, stop=True),
                         reads=[kt, qt], writes=[psa], mark=(h == 3))
                P.op(P.dve, lambda e, d=d, psa=psa, am_=am_: e.tensor_tensor(out=am_[:, d, :], in0=psa[:], in1=gm[:, d, :], op=ALU.mult), reads=[psa, gm], writes=[am_])
            for h in range(4):
                pr = slice((h % 2) * 64, (h % 2 + 1) * 64)
                oc = slice((h // 2) * 128, (h // 2 + 1) * 128)
                for d in range(2):
                    P.op(P.pe, lambda e, d=d, h=h, pr=pr, oc=oc, tm=tm, am_=am_: e.matmul(ps_o[pr, oc], lhsT=tm[:, h * 64:(h + 1) * 64], rhs=am_[:, d, h * 128:(h + 1) * 128], start=(d == 0), stop=False),
                         reads=[tm, am_], writes=[ps_o], mark=False)
                    P.op(P.pe, lambda e, d=d, h=h, pr=pr, oc=oc, qt=qt, n=n: e.matmul(ps_o[pr, oc], lhsT=Sall[pr, n, d * 128 + (h // 2) * 64:d * 128 + (h // 2 + 1) * 64], rhs=qt[pr, d * 2 + h // 2, :], start=False, stop=(d == 1)),
                         reads=[Sall, qt], writes=[ps_o], mark=(d == 1 and h == 3))
            o32, sq, rs_, sg_, yg = o32r.next(), sqr.next(), rsr.next(), sgr.next(), ygr.next()
            P.op(P.act, lambda e, o32=o32: e.activation(out=o32[:], in_=ps_o[:, 0:256], func=AF.Copy), reads=[ps_o], writes=[o32])
            P.op(P.pool, lambda e, o32=o32, sq=sq: e.tensor_tensor(out=sq[:], in0=o32[:], in1=o32[:], op=ALU.mult), reads=[o32], writes=[sq])
            P.op(P.pe, lambda e, sq=sq: e.matmul(ps_m[:, 0:256], lhsT=bones[:], rhs=sq[:], start=True, stop=True), reads=[bones, sq], writes=[ps_m])
            P.op(P.act, lambda e, rs_=rs_: e.activation(out=rs_[:], in_=ps_m[:, 0:256], func=AF.Sqrt, bias=misc[:, 1:2]), reads=[ps_m, misc], writes=[rs_])
            P.op(P.dve, lambda e, rs_=rs_: e.reciprocal(out=rs_[:], in_=rs_[:]), reads=[rs_], writes=[rs_])
            P.op(P.act, lambda e, sg_=sg_, gg=gg: e.activation(out=sg_[:], in_=gg[:].rearrange("p a t -> p (a t)"), func=AF.Silu), reads=[gg], writes=[sg_])
            P.op(P.dve, lambda e, o32=o32, rs_=rs_: e.tensor_tensor(out=o32[:], in0=o32[:], in1=rs_[:], op=ALU.mult), reads=[o32, rs_], writes=[o32])
            P.op(P.dve, lambda e, o32=o32, sg_=sg_, yg=yg: e.scalar_tensor_tensor(out=yg[:].rearrange("p a t -> p (a t)"), in0=o32[:], scalar=ng[:, 0:1], in1=sg_[:], op0=ALU.mult, op1=ALU.mult),
                 reads=[o32, ng, sg_], writes=[yg])
            P.dma(P.sp, catT[6:8][:, :, tk].rearrange("c p t -> p c t"), yg[:], out_tile=catT, in_tile=yg)

    with self.nc.named_scope("B2_conv"):
        uext = P.sbuf("b_uext", [128, 2, NLAT + 2], BF16, pst)
        uctx = P.sbuf("b_uctx", [128, 2, NCTX + 2], BF16, pst)
        uh = P.sbuf("b_uh", [128, 2, 2], BF16, pst)
        btr = P.ring("b_bt", 2, [128, 512], BF16, pst)
        accr = P.ring("b_acc", 2, [128, 512], F32, pst)
        cstg = P.ring("b_cstg", 2, [128, 512], BF16, pst)
        P.dma(P.sp, uh[:], u_halo[:], out_tile=uh)
        P.op(P.pool, lambda e: e.memset(uctx[:], 0.0), writes=[uctx])
        for c in range(2):
            P.dma(P.sp, uext[:, c, 1:NLAT + 1], fmT[C_U + c][:, 0:NLAT], out_tile=uext, in_tile=fmT)
            P.dma(P.sp, uctx[:, c, 1:NCTX + 1], fmT[C_U + c][:, NLAT:NT], out_tile=uctx, in_tile=fmT)
        for c in range(2):
            P.op(P.pool, lambda e, c=c: e.tensor_copy(out=uext[:, c, 0:1], in_=uh[:, c, 0:1]), reads=[uh], writes=[uext])
            P.op(P.pool, lambda e, c=c: e.tensor_copy(out=uext[:, c, NLAT + 1:NLAT + 2], in_=uh[:, c, 1:2]), reads=[uh], writes=[uext])
        for (t0, T, is_ctx) in macro_tiles():
            for c in range(2):
                src = uctx if is_ctx else uext
                o0 = 0 if is_ctx else t0
                bt, acc, st_ = btr.next(), accr.next(), cstg.next()
                P.dma(P.sp, bt[:, 0:T], fmT[C_B + c][:, t0:t0 + T], out_tile=bt, in_tile=fmT)
                P.op(P.act, lambda e, c=c, acc=acc, src=src, o0=o0, T=T: e.activation(out=acc[:, 0:T], in_=src[:, c, o0:o0 + T], func=AF.Copy, scale=cwt[:, c, 0:1]), reads=[src, cwt], writes=[acc])
                for jj in (1, 2):
                    P.op(P.dve, lambda e, c=c, acc=acc, src=src, o0=o0, T=T, jj=jj: e.scalar_tensor_tensor(out=acc[:, 0:T], in0=src[:, c, o0 + jj:o0 + jj + T], scalar=cwt[:, c, jj:jj + 1], in1=acc[:, 0:T],
                                                                                                   op0=ALU.mult, op1=ALU.add), reads=[src, cwt, acc], writes=[acc])
                P.op(P.pool, lambda e, acc=acc, bt=bt, st_=st_, T=T: e.tensor_tensor(out=st_[:, 0:T], in0=acc[:, 0:T], in1=bt[:, 0:T], op=ALU.mult), reads=[acc, bt], writes=[st_])
                P.dma(P.sp, catT[c][:, t0:t0 + T], st_[:, 0:T], out_tile=catT, in_tile=st_)

    with self.nc.named_scope("B3_attn"):
        kTall = P.sbuf("b_kT", [128, 2, NLAT + 256], BF16, pst)
        kTc = P.sbuf("b_kTc", [128, 2, NCTX], BF16, pst)
        vall = P.sbuf("b_v", [128, 34, 256], BF16, pst)
        vctx = P.sbuf("b_vc", [128, 2, 256], BF16, pst)
        qr_ = P.ring("b_q", 2, [128, 4, 128], BF16, pst)
        ptr = P.ring("b_pt", 2, [128, 5, 512], BF16, pst)
        denr = P.ring("b_den", 2, [128, 512], F32, pst)
        astg = P.ring("b_astg", 2, [128, 2, 128], BF16, pst)
        for g in range(2):
            P.dma(P.sp, kTall[:, g, 128:128 + NLAT], fmT[C_K + g][:, 0:NLAT], out_tile=kTall, in_tile=fmT)
            P.dma(P.sp, kTc[:, g, :], fmT[C_K + g][:, NLAT:NT], out_tile=kTc, in_tile=fmT)
        P.dma(P.sp, kTall[:, :, 0:128], kT_halo[:, :, 0:128], out_tile=kTall)
        P.dma(P.sp, kTall[:, :, 128 + NLAT:256 + NLAT], kT_halo[:, :, 128:256], out_tile=kTall)
        P.dma(P.sp, vall[:, 1:33, :], tmS[0:NLAT, 512:768].rearrange("(n p) c -> p n c", p=128), out_tile=vall, in_tile=tmS)
        P.dma(P.sp, vall[:, 0, :], v_halo[:, 0, :], out_tile=vall)
        P.dma(P.sp, vall[:, 33, :], v_halo[:, 1, :], out_tile=vall)
        P.dma(P.sp, vctx[:], tmS[NLAT:NT, 512:768].rearrange("(n p) c -> p n c", p=128), out_tile=vctx, in_tile=tmS)
        ps_s = [self.psb[i] for i in range(5)]
        ps_pv, ps_dn = self.psb[5], self.psb[6]
        for i in range(NCH):
            is_ctx = i >= 32
            tk = slice(i * 128, (i + 1) * 128)
            q = qr_.next()
            P.dma(P.sp, q[:], fmT[C_Q:C_Q + 4][:, :, tk].rearrange("c p t -> p c t"), out_tile=q, in_tile=fmT)
            for g in range(2):
                keys = [(kTc, slice(0, 128), vctx, 0, None), (kTc, slice(128, 256), vctx, 1, None)]
                if not is_ctx:
                    keys.append((kTall, slice(i * 128, (i + 1) * 128), vall, i, 0 if i == 0 else 1))
                    keys.append((kTall, slice((i + 1) * 128, (i + 2) * 128), vall, i + 1, None))
                    keys.append((kTall, slice((i + 2) * 128, (i + 3) * 128), vall, i + 2, 3 if i == 31 else 2))
                pt = ptr.next()
                nk = len(keys)
                for kc, (kt_, ks, vt_, vi, mk) in enumerate(keys):
                    for hh in range(4):
                        pr = slice((hh % 2) * 64, (hh % 2 + 1) * 64)
                        P.op(P.pe, lambda e, kc=kc, hh=hh, pr=pr, kt_=kt_, ks=ks, q=q, g=g: e.matmul(ps_s[kc][:, hh * 128:(hh + 1) * 128], lhsT=kt_[pr, g, ks], rhs=q[pr, 2 * g + hh // 2, :], start=True, stop=True),
                             reads=[kt_, q], writes=[ps_s[kc]], mark=(hh == 3))
                    P.op(P.act, lambda e, kc=kc, pt=pt: e.activation(out=pt[:, kc, :], in_=ps_s[kc][:], func=AF.Exp, scale=0.125), reads=[ps_s[kc]], writes=[pt])
                    if mk is not None:
                        P.op(P.dve, lambda e, kc=kc, pt=pt, mk=mk: e.tensor_tensor(out=pt[:, kc, :], in0=pt[:, kc, :], in1=am[:, mk, :], op=ALU.mult), reads=[pt, am], writes=[pt])
                for kc, (kt_, ks, vt_, vi, mk) in enumerate(keys):
                    P.op(P.pe, lambda e, kc=kc, vt_=vt_, vi=vi, pt=pt, g=g: e.matmul(ps_pv[:], lhsT=vt_[:, vi, g * 128:(g + 1) * 128], rhs=pt[:, kc, :], start=(kc == 0), stop=(kc == nk - 1)),
                         reads=[vt_, pt], writes=[ps_pv], mark=(kc == nk - 1))
                for kc in range(nk):
                    P.op(P.pe, lambda e, kc=kc, pt=pt: e.matmul(ps_dn[:], lhsT=onesb[:], rhs=pt[:, kc, :], start=(kc == 0), stop=(kc == nk - 1)),
                         reads=[onesb, pt], writes=[ps_dn], mark=(kc == nk - 1))
                den, st_ = denr.next(), astg.next()
                P.op(P.dve, lambda e, den=den, g=g: e.tensor_tensor(out=den[:], in0=ps_dn[:], in1=sexp[:, g, :], op=ALU.add), reads=[ps_dn, sexp], writes=[den])
                P.op(P.dve, lambda e, den=den: e.reciprocal(out=den[:], in_=den[:]), reads=[den], writes=[den])
                for hh in range(4):
                    pr = slice((hh % 2) * 64, (hh % 2 + 1) * 64)
                    P.op(P.dve, lambda e, hh=hh, pr=pr, den=den, st_=st_: e.tensor_tensor(out=st_[pr, hh // 2, :], in0=ps_pv[pr, hh * 128:(hh + 1) * 128], in1=den[pr, hh * 128:(hh + 1) * 128], op=ALU.mult),
                         reads=[ps_pv, den], writes=[st_])
                P.dma(P.sp, catT[2 + 2 * g:4 + 2 * g][:, :, tk].rearrange("c p t -> p c t"), st_[:], out_tile=catT, in_tile=st_)


Builder.phase_B = _phase_B


def _phase_O(self):
    P, pst = self.P, self.pst
    l = self.layer = 0
    catT = self.dio("catT", [8, 128, NT], BF16, "B" not in self.phases, False)
    xres = self.dio("xres", [NT, D], F32, True, False)
    self.modrow = self.dio("modrow", [DEPTH, 2, 6 * D], F32, True, False)
    w_out = self.din("w_out", [D, D])
    x1 = self.dio("x1", [NT, D], F32, False, "F" not in self.phases)
    wo = P.sbuf("o_w", [128, 8, D], BF16, pst)
    GT = P.sbuf("o_gt", [128, D], F32, pst)
    catr = P.ring("o_cat", 2, [128, 8, 512], BF16, pst)
    xr = P.ring("o_x", 2, [128, 4, D], F32, pst)
    tmpr = P.ring("o_tmp", 2, [128, 512], F32, pst)
    self.load_mod_tiles([GT], [2], 0, None)
    self.load_weight_bf16(wo, w_out, D)
    pss = Ring([self.psb[0], self.psb[1], self.psb[2], self.psb[3]])
    for (t0, T, is_ctx) in macro_tiles():
        nt = T // 128
        if is_ctx:
            self.load_mod_tiles([GT], [2], 1, None)
        cat, xt = catr.next(), xr.next()
        P.dma(P.sp, cat[:, :, 0:T], catT[:, :, t0:t0 + T].rearrange("c p t -> p c t"), out_tile=cat, in_tile=catT)
        P.dma(P.sp, xt[:, 0:nt, :], xres[t0:t0 + T, :].rearrange("(s p) f -> p s f", p=128), out_tile=xt, in_tile=xres)
        for s in range(nt):
            for hf in range(2):
                ps = pss.next()
                cs = slice(hf * 512, (hf + 1) * 512)
                for mc in range(8):
                    P.op(P.pe, lambda e, mc=mc, ps=ps, cs=cs, s=s, cat=cat: e.matmul(ps[:], lhsT=cat[:, mc, s * 128:(s + 1) * 128], rhs=wo[:, mc, cs], start=(mc == 0), stop=(mc == 7)),
                         reads=[cat, wo], writes=[ps], mark=(mc == 7))
                tmp = tmpr.next()
                P.op(P.dve, lambda e, ps=ps, cs=cs, tmp=tmp: e.tensor_tensor(out=tmp[:], in0=ps[:], in1=GT[:, cs], op=ALU.mult), reads=[ps, GT], writes=[tmp])
                P.op(P.pool, lambda e, cs=cs, s=s, tmp=tmp, xt=xt: e.tensor_tensor(out=xt[:, s, cs], in0=xt[:, s, cs], in1=tmp[:], op=ALU.add), reads=[xt, tmp], writes=[xt])
        P.dma(P.sp, x1[t0:t0 + T, :].rearrange("(s p) f -> p s f", p=128), xt[:, 0:nt, :], out_tile=x1, in_tile=xt)


def _phase_F(self):
    P, pst = self.P, self.pst
    l = self.layer = 0
    x1 = self.dio("x1", [NT, D], F32, "O" not in self.phases, False)
    self.modrow = self.dio("modrow", [DEPTH, 2, 6 * D], F32, True, False)
    w_up = self.din("w_up", [D, 2 * DFF])
    w_dn = self.din("w_down", [DFF, D])
    g2 = self.din("norm2_g", [1, D])
    c_misc = self.din("c_misc", [128, 8])
    xout = self.dout("xout", [NT, D])
    wu = P.sbuf("f_wu", [128, 8, 2 * DFF], BF16, pst)
    wd = P.sbuf("f_wd", [128, 22, D], BF16, pst)
    misc = P.sbuf("f_misc", [128, 8], F32, pst)
    self.epsb = misc
    G2 = P.sbuf("f_G2", [128, D], F32, pst)
    SH2 = P.sbuf("f_SH2", [128, D], F32, pst)
    GT2 = P.sbuf("f_GT2", [128, D], F32, pst)
    self.gtile = P.sbuf("f_gt", [128, D], F32, pst)
    xr = P.ring("f_x", 2, [128, 2, D], F32, pst)
    xnTr = P.ring("f_xnT", 2, [128, 8, 256], BF16, pst)
    stat = P.ring("f_stat", 2, [128, 4], F32, pst)
    xn32 = P.sbuf("f_xn32", [128, D], F32, pst)
    xnb = P.ring("f_xnb", 2, [128, D], BF16, pst)
    hT = P.sbuf("f_hT", [128, 22, 256], BF16, pst)
    sar = P.ring("f_sa", 2, [128, 256], F32, pst)
    tmpr = P.ring("f_tmp", 2, [128, 512], F32, pst)
    P.dma(P.sp, misc[:], c_misc[:], out_tile=misc)
    self.load_mod_tiles([G2, SH2, GT2], [4, 3, 5], 0, g2[0:1, :])
    self.load_weight_bf16(wu, w_up, 2 * DFF, split=2)
    self.load_weight_bf16(wd, w_dn, D, kchunks=22)
    if self.last:
        fg = self.din("final_g", [1, D])
        FG = self.gtile
        P.dma(P.sp, FG[:], fg[0:1, :].partition_broadcast(128), out_tile=FG)
    psab = Ring([self.psb[0], self.psb[1], self.psb[2], self.psb[3]])
    psy = Ring([self.psb[4], self.psb[5]])
    T = 256
    mts = [(i * T, T, i * T >= NLAT) for i in range(NT // T)]
    loaded = {}

    def load_mt(i):
        t0 = mts[i][0]
        xt = xr.next()
        P.dma(P.sp, xt[:], x1[t0:t0 + T, :].rearrange("(s p) f -> p s f", p=128), out_tile=xt, in_tile=x1)
        loaded[i] = xt

    load_mt(0)
    for mi, (t0, T_, is_ctx) in enumerate(mts):
        if mi + 1 < len(mts):
            load_mt(mi + 1)
        if is_ctx:
            self.load_mod_tiles([G2, SH2, GT2], [4, 3, 5], 1, g2[0:1, :])
        xt = loaded.pop(mi)
        xnT = xnTr.next()
        for s in range(2):
            self.norm_transpose(xt, s, G2, SH2, xnT, s * 128, stat.next(), xn32, xn32, xnb.next())
        for fc in range(22):
            pa, pb = psab.next(), psab.next()
            for (pp, c0) in ((pa, fc * 128), (pb, DFF + fc * 128)):
                for k in range(8):
                    P.op(P.pe, lambda e, k=k, pp=pp, c0=c0, xnT=xnT: e.matmul(pp[:, 0:T], lhsT=wu[:, k, c0:c0 + 128], rhs=xnT[:, k, :], start=(k == 0), stop=(k == 7)),
                         reads=[wu, xnT], writes=[pp], mark=(k == 7))
            sa = sar.next()
            P.op(P.act, lambda e, pa=pa, sa=sa: e.activation(out=sa[:], in_=pa[:, 0:T], func=AF.Silu), reads=[pa], writes=[sa])
            P.op(P.dve, lambda e, pb=pb, sa=sa, fc=fc: e.tensor_tensor(out=hT[:, fc, :], in0=pb[:, 0:T], in1=sa[:], op=ALU.mult), reads=[pb, sa], writes=[hT])
        for s in range(2):
            for hf in range(2):
                cs = slice(hf * 512, (hf + 1) * 512)
                ps = psy.next()
                for fc in range(22):
                    P.op(P.pe, lambda e, fc=fc, ps=ps, cs=cs, s=s: e.matmul(ps[:], lhsT=hT[:, fc, s * 128:(s + 1) * 128], rhs=wd[:, fc, cs], start=(fc == 0), stop=(fc == 21)),
                         reads=[hT, wd], writes=[ps], mark=(fc == 21))
                tmp = tmpr.next()
                P.op(P.dve, lambda e, ps=ps, cs=cs, tmp=tmp: e.tensor_tensor(out=tmp[:], in0=ps[:], in1=GT2[:, cs], op=ALU.mult), reads=[ps, GT2], writes=[tmp])
                P.op(P.pool, lambda e, cs=cs, s=s, tmp=tmp, xt=xt: e.tensor_tensor(out=xt[:, s, cs], in0=xt[:, s, cs], in1=tmp[:], op=ALU.add), reads=[xt, tmp], writes=[xt])
            if self.last and not is_ctx:
                st_ = stat.next()
                P.op(P.act, lambda e, s=s, xt=xt, st_=st_: e.activation(out=xn32[:], in_=xt[:, s, :], func=AF.Square, accum_out=st_[:, 0:1]), reads=[xt], writes=[xn32, st_])
                P.op(P.act, lambda e, st_=st_: e.activation(out=st_[:, 1:2], in_=st_[:, 0:1], func=AF.Sqrt, scale=1.0 / D, bias=misc[:, 1:2]), reads=[st_, misc], writes=[st_])
                P.op(P.dve, lambda e, st_=st_: e.reciprocal(out=st_[:, 2:3], in_=st_[:, 1:2]), reads=[st_], writes=[st_])
                P.op(P.dve, lambda e, s=s, xt=xt, st_=st_: e.scalar_tensor_tensor(out=xt[:, s, :], in0=xt[:, s, :], scalar=st_[:, 2:3], in1=FG[:], op0=ALU.mult, op1=ALU.mult),
                     reads=[xt, st_, FG], writes=[xt])
        P.dma(P.sp, xout[t0:t0 + T, :].rearrange("(s p) f -> p s f", p=128), xt[:], out_tile=xout, in_tile=xt)


Builder.phase_O = _phase_O
Builder.phase_F = _phase_F


def const_inputs_B():
    s = np.arange(128)
    c = {}
    g = np.float32(-1.0 / 16.0)
    m2 = np.zeros((128, 2, 128), np.float32)
    m2[:, 0, :] = np.where(s[:, None] <= s[None, :], g, 0)
    m2[:, 1, :] = np.where(s[:, None] >= s[None, :], g, 0)
    c["c_m2"] = m2
    bo = np.zeros((128, 128), np.float32)
    bo[0:64, 0:64] = 1.0 / 64
    bo[64:128, 64:128] = 1.0 / 64
    c["c_bones"] = bo
    gm = np.zeros((128, 2, 4, 128), np.float32)
    gm[:, 0] = (s[:, None] <= s[None, :]).astype(np.float32)[:, None, :]
    gm[:, 1] = (s[:, None] >= s[None, :]).astype(np.float32)[:, None, :]
    c["gmask"] = gm.reshape(128, 2, 512)
    return c


def attn_masks(seg):
    s = np.arange(128)
    mp = (s[:, None] >= s[None, :]).astype(np.float32)
    mn = (s[:, None] <= s[None, :]).astype(np.float32)
    am = np.zeros((128, 4, 4, 128), np.float32)
    am[:, 0] = (mp if seg > 0 else np.zeros_like(mp))[:, None, :]
    am[:, 1] = mp[:, None, :]
    am[:, 2] = mn[:, None, :]
    am[:, 3] = (mn if seg < NSEG - 1 else np.zeros_like(mn))[:, None, :]
    return am.reshape(128, 4, 512)


def layer_inputs_B(inp, l):
    d = {}
    d["conv_cw"] = np.ascontiguousarray(inp["conv_w"][l].reshape(3, 2, 128).transpose(2, 1, 0))
    d["sink_row"] = inp["attn_sink"][l][None, :]
    d["gla_ng"] = np.concatenate([inp["gla_norm_g"][l], inp["gla_norm_g"][l]])[:, None]
    d["w_out"] = inp["w_out"][l]
    d["w_up"] = inp["w_up"][l]
    d["w_down"] = inp["w_down"][l]
    d["norm2_g"] = inp["norm2_g"][l][None, :]
    return d


_PROGS = {}


def get_prog(phases, last=False):
    key = (tuple(phases), last)
    if key not in _PROGS:
        b = Builder(list(phases), last=last)
        _PROGS[key] = b.build()
    return _PROGS[key]


def kernel(x, c, ctx, c_ctx, w_mod, b_mod, norm1_g, norm2_g, w_in, conv_w, attn_sink,
           gla_gate_w, gla_gate_b, gla_norm_g, w_out, w_up, w_down, final_norm_g):
    inp = dict(x=x, c=c, ctx=ctx, c_ctx=c_ctx, w_mod=w_mod, b_mod=b_mod, norm1_g=norm1_g, norm2_g=norm2_g, w_in=w_in,
               conv_w=conv_w, attn_sink=attn_sink, gla_gate_w=gla_gate_w, gla_gate_b=gla_gate_b, gla_norm_g=gla_norm_g,
               w_out=w_out, w_up=w_up, w_down=w_down, final_norm_g=final_norm_g)
    inp = {k: np.ascontiguousarray(np.asarray(v, dtype=np.float32)) for k, v in inp.items()}
    ncore = 8
    ids = list(range(ncore))
    consts = const_inputs()
    constsB = const_inputs_B()
    ropes = [rope_tables(j) for j in range(NSEG)]
    resM = run_bass_kernel_spmd(get_prog(["M"]), [dict(c_ident=consts["c_ident"], **core_inputs_M(inp, cid // 4)) for cid in ids], core_ids=ids)
    modrows = [np.asarray(r["modrow"]) for r in resM.results]
    xres = [np.concatenate([inp["x"][cid // 4, (cid % 4) * NLAT:(cid % 4 + 1) * NLAT], inp["ctx"][cid // 4]], 0) for cid in ids]
    for l in range(DEPTH):
        la = layer_inputs_A(inp, l)
        mapsA = []
        for cid in ids:
            m = dict(consts)
            m.update(la)
            m["modrow"] = np.ascontiguousarray(np.roll(modrows[cid], -l, axis=0))
            m["xres"] = xres[cid]
            m["cosT"], m["sinT"] = ropes[cid % 4]
            mapsA.append(m)
        resA = run_bass_kernel_spmd(get_prog(["A"]), mapsA, core_ids=ids)
        RA = [{k: np.asarray(v) for k, v in r.items()} for r in resA.results]
        lb = layer_inputs_B(inp, l)
        mapsB = []
        for cid in ids:
            b, j = cid // 4, cid % 4
            ra = RA[cid]
            m = dict(c_ident=consts["c_ident"], c_misc=consts["c_misc"])
            m.update(constsB)
            m.update(lb)
            for k in ("fmT", "tmS", "spS", "ckv", "cdd", "summ"):
                m[k] = ra[k]
            m["summ_all"] = np.ascontiguousarray(np.stack([RA[b * 4 + i]["summ"][:, 0, :] for i in range(4)], axis=1))
            cmk = np.zeros((128, 8), np.float32)
            for i in range(4):
                cmk[:, i] = 1.0 if i < j else 0.0
                cmk[:, 4 + i] = 1.0 if i > j else 0.0
            m["chainmask"] = cmk
            kh = np.zeros((128, 2, 256), ra["fmT"].dtype)
            vh = np.zeros((128, 2, 256), ra["tmS"].dtype)
            uh = np.zeros((128, 2, 2), ra["fmT"].dtype)
            if j > 0:
                rp = RA[cid - 1]
                kh[:, :, 0:128] = rp["fmT"][C_K:C_K + 2, :, NLAT - 128:NLAT].transpose(1, 0, 2)
                vh[:, 0, :] = rp["tmS"][NLAT - 128:NLAT, 512:768]
                uh[:, :, 0] = rp["fmT"][C_U:C_U + 2, :, NLAT - 1].T
            if j < NSEG - 1:
                rn = RA[cid + 1]
                kh[:, :, 128:256] = rn["fmT"][C_K:C_K + 2, :, 0:128].transpose(1, 0, 2)
                vh[:, 1, :] = rn["tmS"][0:128, 512:768]
                uh[:, :, 1] = rn["fmT"][C_U:C_U + 2, :, 0].T
            m["kT_halo"], m["v_halo"], m["u_halo"] = kh, vh, uh
            m["amask"] = attn_masks(j)
            m["xres"] = xres[cid]
            m["modrow"] = np.ascontiguousarray(np.roll(modrows[cid], -l, axis=0))
            last = (l == DEPTH - 1)
            if last:
                m["final_g"] = inp["final_norm_g"][None, :]
            mapsB.append(m)
        resB = run_bass_kernel_spmd(get_prog(["B", "O", "F"], last=(l == DEPTH - 1)), mapsB, core_ids=ids)
        xres = [np.asarray(r["xout"]) for r in resB.results]
    out = np.zeros((NB, SEQ, D), np.float32)
    for cid in ids:
        out[cid // 4, (cid % 4) * NLAT:(cid % 4 + 1) * NLAT] = xres[cid][:NLAT]
    return out
```

```python
import numpy as np
from contextlib import ExitStack
import concourse.bass as bass
import concourse.mybir as mybir
from concourse.bass_utils import run_bass_kernel_spmd

F32 = mybir.dt.float32
BF16 = mybir.dt.bfloat16
AF = mybir.ActivationFunctionType
ALU = mybir.AluOpType
AX = mybir.AxisListType

D = 1024
SEQ = 16384
NB = 2
DEPTH = 4
NSEG = 4
NLAT = SEQ // NSEG
NCTX = 256
NT = NLAT + NCTX
NCH = NT // 128
DFF = 2816
EPS = 1e-6
NFM = 3104
NTM = 768
C_U, C_B, C_Q, C_K, C_GQ, C_GK, C_GG = 0, 2, 4, 8, 10, 12, 14
NFMO = 16
FUSED = False


class Ev:
    __slots__ = ("kind", "src", "val")

    def __init__(self, kind, src, val):
        self.kind = kind
        self.src = src
        self.val = val


class EngState:
    def __init__(self, prog, name, eng):
        self.prog = prog
        self.name = name
        self.eng = eng
        self.sem = prog.new_sem("e_" + name)
        self.seq = 0
        self.count = 0
        self.marks = []
        self.last = None
        self.last_marked = True
        self.seen = {}

    def mark_last(self):
        if not self.last_marked and self.last is not None:
            self.last.then_inc(self.sem, 1)
            self.count += 1
            self.marks.append((self.seq, self.count))
            self.last_marked = True

    def value_for(self, seq):
        ms = self.marks
        lo, hi = 0, len(ms)
        while lo < hi:
            mid = (lo + hi) // 2
            if ms[mid][0] >= seq:
                hi = mid
            else:
                lo = mid + 1
        if lo == len(ms):
            self.mark_last()
            return self.marks[-1][1]
        return ms[lo][1]

    def wait(self, ev):
        if ev is None:
            return
        if ev.kind == "eng":
            src = ev.src
            if ev.val <= 0:
                return
            val = src.value_for(ev.val)
            sem = src.sem
        else:
            sem = ev.src.sem
            val = ev.val
        key = id(sem)
        if self.seen.get(key, 0) >= val:
            return
        self.eng.wait_ge(sem, val)
        self.seen[key] = val


class DmaSem:
    def __init__(self, prog, name):
        self.sem = prog.new_sem("d_" + name)
        self.count = 0


class Tile:
    def __init__(self, prog, name, t=None):
        self.prog = prog
        self.name = name
        self.t = t
        self.w = None
        self.r = {}
        self.dsem = None
        self.is_dram = False
        self.persistent = not getattr(prog, "in_phase", False)

    def __getitem__(self, idx):
        return self.t[idx]

    def get_dsem(self):
        if self.dsem is None:
            pr = self.prog
            if not (self.persistent or self.is_dram) and pr.free_dsems:
                self.dsem = pr.free_dsems.pop()
            else:
                self.dsem = DmaSem(pr, self.name)
                pr.dsems.append(self.dsem)
            if not (self.persistent or self.is_dram):
                pr.phase_dsems.append(self.dsem)
        return self.dsem


class Ring:
    def __init__(self, tiles):
        self.tiles = tiles
        self.i = 0

    def next(self):
        t = self.tiles[self.i % len(self.tiles)]
        self.i += 1
        return t


class Prog:
    def __init__(self, nc, stack):
        self.nc = nc
        self.stack = stack
        self.nsem = 0
        self.dsems = []
        self.free_dsems = []
        self.phase_dsems = []
        self.in_phase = False
        self.suffix = ""
        self.pe = EngState(self, "pe", nc.tensor)
        self.act = EngState(self, "act", nc.scalar)
        self.dve = EngState(self, "dve", nc.vector)
        self.pool = EngState(self, "pool", nc.gpsimd)
        self.sp = EngState(self, "sp", nc.sync)
        self.engs = [self.pe, self.act, self.dve, self.pool, self.sp]

    def new_sem(self, name):
        self.nsem += 1
        return self.stack.enter_context(self.nc.semaphore(name + "_%d" % self.nsem))

    def sbuf(self, name, shape, dt, stack=None):
        st = stack if stack is not None else self.stack
        name = name + self.suffix
        t = st.enter_context(self.nc.sbuf_tensor(name, list(shape), dt))
        return Tile(self, name, t)

    def ring(self, name, n, shape, dt, stack=None):
        return Ring([self.sbuf("%s%d" % (name, i), shape, dt, stack) for i in range(n)])

    def psum(self, name, shape, dt, stack=None):
        st = stack if stack is not None else self.stack
        t = st.enter_context(self.nc.psum_tensor(name, list(shape), dt))
        return Tile(self, name, t)

    def dram(self, name, shape, dt, kind="Internal"):
        t = self.nc.dram_tensor(name, list(shape), dt, kind=kind)
        tl = Tile(self, name, t.ap())
        tl.is_dram = True
        return tl

    def op(self, E, fn, reads=(), writes=(), mark=True, serial=False):
        pe = self.pe

        def need(ev):
            return serial or not (E is pe and ev.kind == "eng" and ev.src is pe)

        for t in reads:
            if t.w is not None and need(t.w):
                E.wait(t.w)
        for t in writes:
            if t.w is not None and need(t.w):
                E.wait(t.w)
            for ev in t.r.values():
                if need(ev):
                    E.wait(ev)
        inst = fn(E.eng)
        E.seq += 1
        E.last = inst
        E.last_marked = False
        if mark:
            E.mark_last()
        ev = Ev("eng", E, E.seq)
        for t in reads:
            t.r[id(E)] = ev
        for t in writes:
            t.w = ev
            t.r = {}
        return inst

    def dma(self, Q, out_ap, in_ap, out_tile=None, in_tile=None, group=None, **kw):
        if in_tile is not None and in_tile.w is not None:
            Q.wait(in_tile.w)
        if out_tile is not None and not out_tile.is_dram:
            if out_tile.w is not None:
                Q.wait(out_tile.w)
            for ev in out_tile.r.values():
                Q.wait(ev)
        g = group if group is not None else out_tile
        ds = g.get_dsem()
        inst = Q.eng.dma_start(out=out_ap, in_=in_ap, **kw)
        inst.then_inc(ds.sem, 16)
        ds.count += 16
        Q.seq += 1
        Q.last = inst
        Q.last_marked = True
        ev = Ev("dma", ds, ds.count)
        if in_tile is not None:
            in_tile.r[id(ds)] = ev
        if out_tile is not None:
            out_tile.w = ev
            out_tile.r = {}
        return inst

    def end_phase(self):
        self.barrier()
        self.free_dsems.extend(self.phase_dsems)
        self.phase_dsems = []

    def barrier(self):
        for E in self.engs:
            E.mark_last()
        for F in self.engs:
            for E in self.engs:
                if E is F or E.count == 0:
                    continue
                if F.seen.get(id(E.sem), 0) < E.count:
                    F.eng.wait_ge(E.sem, E.count)
                    F.seen[id(E.sem)] = E.count
            for ds in self.dsems:
                if ds.count and F.seen.get(id(ds.sem), 0) < ds.count:
                    F.eng.wait_ge(ds.sem, ds.count)
                    F.seen[id(ds.sem)] = ds.count


def macro_tiles():
    mts = [(i * 512, 512, False) for i in range(NLAT // 512)]
    mts.append((NLAT, NCTX, True))
    return mts


class Builder:
    LAYER_IN = ("w_in_fm", "w_in_tm", "norm1_g", "gate_w", "gate_b", "conv_cw", "sink_row", "gla_ng", "w_out", "w_up", "w_down", "norm2_g")
    SEG_IN = ("cosT", "sinT", "amask", "chainmask")
    SEG_SCR = ("fmT", "tmS", "spS", "ckv", "cdd", "summ", "catT", "x1")

    def __init__(self, phases, last=False, fused=False):
        self.nc = bass.Bass("TRN2", target_bir_lowering=False)
        self.phases = phases if not fused else ["M", "A", "B", "O", "F"]
        self.last = last
        self.fused = fused
        self.layer = 0
        self.seg = 0
        self.io = {}

    def full(self, name, shape, dt, kind):
        if name not in self.io:
            self.io[name] = self.P.dram(name, shape, dt, kind=kind)
        return self.io[name]

    def view(self, full, idx):
        v = Tile(self.P, full.name + "_v", full.t[idx])
        v.is_dram = True
        v.dsem = full.get_dsem()
        return v

    def din(self, name, shape, dt=F32):
        if self.fused and name in self.LAYER_IN:
            return self.view(self.full(name, [DEPTH] + list(shape), dt, "ExternalInput"), self.layer)
        if self.fused and name in self.SEG_IN:
            return self.view(self.full(name, [NSEG] + list(shape), dt, "ExternalInput"), self.seg)
        if name in self.io:
            return self.io[name]
        t = self.P.dram(name, shape, dt, kind="ExternalInput")
        self.io[name] = t
        return t

    def dout(self, name, shape, dt=F32):
        if self.fused:
            assert name == "xout"
            if self.last:
                return self.view(self.full("yout", [NSEG] + list(shape), dt, "ExternalOutput"), self.seg)
            return self.view(self.full("xcur", [NSEG] + list(shape), dt, "Internal"), self.seg)
        t = self.P.dram(name, shape, dt, kind="ExternalOutput")
        self.io[name] = t
        return t

    def dio(self, name, shape, dt, is_in, is_out):
        if self.fused:
            if name == "modrow":
                return self.full("modrow", shape, dt, "Internal")
            if name == "xres":
                if self.layer == 0:
                    return self.view(self.full("xres", [NSEG] + list(shape), dt, "ExternalInput"), self.seg)
                return self.view(self.full("xcur", [NSEG] + list(shape), dt, "Internal"), self.seg)
            assert name in self.SEG_SCR, name
            return self.view(self.full(name, [NSEG] + list(shape), dt, "Internal"), self.seg)
        if name in self.io:
            return self.io[name]
        if is_in:
            return self.din(name, shape, dt)
        if is_out:
            return self.dout(name, shape, dt)
        t = self.P.dram(name, shape, dt)
        self.io[name] = t
        return t

    def build(self):
        nc = self.nc
        with ExitStack() as st:
            self.P = P = Prog(nc, st)
            self.consts(st)
            if not self.fused:
                sched = [(ph, 0, 0, self.last) for ph in self.phases]
            else:
                sched = [("M", 0, 0, False)]
                for l in range(DEPTH):
                    sched += [("A", l, sg, False) for sg in range(NSEG)]
                    for sg in range(NSEG):
                        sched += [(ph, l, sg, l == DEPTH - 1) for ph in ("B", "O", "F")]
            for i, (ph, l, sg, last) in enumerate(sched):
                self.layer, self.seg, self.last = l, sg, last
                P.in_phase = True
                P.suffix = "_i%d" % i if self.fused else ""
                with ExitStack() as pst, nc.named_scope("ph_" + ph):
                    self.pst = pst
                    getattr(self, "phase_" + ph)()
                    P.end_phase()
                P.in_phase = False
            P.barrier()
        return nc

    def consts(self, st):
        P = self.P
        self.c_ident = self.din("c_ident", [128, 128])
        self.ident = P.sbuf("ident", [128, 128], BF16)
        P.dma(P.pool, self.ident[:], self.c_ident[:], out_tile=self.ident)
        self.psb = [P.psum("psb%d" % i, [128, 512], F32) for i in range(7)]
        self.pstp = P.psum("pstp", [128, 1024], BF16)

    def phase_M(self):
        P, pst = self.P, self.pst
        cT = self.din("cT", [128, 8, 2])
        wmod = self.din("w_mod", [DEPTH, D, 6 * D])
        bmod = self.din("b_mod2", [DEPTH, 2, 6 * D])
        modrow = self.dio("modrow", [DEPTH, 2, 6 * D], F32, False, "A" not in self.phases)
        cs = P.sbuf("m_cs", [128, 8, 2], F32, pst)
        sg = P.sbuf("m_sg", [128, 8, 2], F32, pst)
        bm = P.sbuf("m_bm", [2, 6 * D], F32, pst)
        mr = P.sbuf("m_mr", [2, 6 * D], F32, pst)
        wr = P.ring("m_w", 3, [128, 8, 512], F32, pst)
        P.dma(P.sp, cs[:], cT[:], out_tile=cs)
        P.op(P.act, lambda e: e.activation(out=sg[:], in_=cs[:], func=AF.Sigmoid), reads=[cs], writes=[sg])
        P.op(P.dve, lambda e: e.tensor_tensor(out=cs[:], in0=cs[:], in1=sg[:], op=ALU.mult), reads=[cs, sg], writes=[cs])
        for l in range(DEPTH):
            P.dma(P.sp, bm[:], bmod[l], out_tile=bm)
            for n in range(12):
                w = wr.next()
                P.dma(P.sp, w[:], wmod[l][:, n * 512:(n + 1) * 512].rearrange("(k p) n -> p k n", p=128), out_tile=w)
                ps = self.psb[n % 2]
                for k in range(8):
                    P.op(P.pe, lambda e, k=k: e.matmul(ps[0:2, :], lhsT=cs[:, k, :], rhs=w[:, k, :], start=(k == 0), stop=(k == 7)),
                         reads=[cs, w], writes=[ps], mark=(k == 7))
                P.op(P.dve, lambda e: e.tensor_tensor(out=mr[:, n * 512:(n + 1) * 512], in0=ps[0:2, :], in1=bm[:, n * 512:(n + 1) * 512], op=ALU.add),
                     reads=[ps, bm], writes=[mr])
            P.dma(P.sp, modrow[l], mr[:], out_tile=modrow, in_tile=mr)

    def load_mod_tiles(self, tiles, idxs, r, gain_row):
        P = self.P
        l = self.layer
        for t, i in zip(tiles, idxs):
            P.dma(P.sp, t[:], self.modrow[l][r:r + 1, i * D:(i + 1) * D].partition_broadcast(128), out_tile=t, in_tile=self.modrow)
        if gain_row is not None:
            g = self.gtile
            P.dma(P.sp, g[:], gain_row.partition_broadcast(128), out_tile=g)
            t = tiles[0]
            P.op(P.dve, lambda e: e.scalar_tensor_tensor(out=t[:], in0=t[:], scalar=1.0, in1=g[:], op0=ALU.add, op1=ALU.mult),
                 reads=[t, g], writes=[t])

    def norm_transpose(self, xt, s, G, SH, xnT, col0, stat, junk, xn32, xnb):
        P = self.P
        P.op(P.act, lambda e: e.activation(out=junk[:], in_=xt[:, s, :], func=AF.Square, accum_out=stat[:, 0:1]), reads=[xt], writes=[junk, stat])
        P.op(P.act, lambda e: e.activation(out=stat[:, 1:2], in_=stat[:, 0:1], func=AF.Sqrt, scale=1.0 / D, bias=self.epsb[:, 1:2]), reads=[stat, self.epsb], writes=[stat])
        P.op(P.dve, lambda e: e.reciprocal(out=stat[:, 2:3], in_=stat[:, 1:2]), reads=[stat], writes=[stat])
        P.op(P.dve, lambda e: e.scalar_tensor_tensor(out=xn32[:], in0=xt[:, s, :], scalar=stat[:, 2:3], in1=G[:], op0=ALU.mult, op1=ALU.mult),
             reads=[xt, stat, G], writes=[xn32])
        P.op(P.pool, lambda e: e.tensor_tensor(out=xnb[:], in0=xn32[:], in1=SH[:], op=ALU.add), reads=[xn32, SH], writes=[xnb])
        tp = self.pstp
        for k in range(8):
            P.op(P.pe, lambda e, k=k: e.transpose(out=tp[:, k * 128:(k + 1) * 128], in_=xnb[:, k * 128:(k + 1) * 128], identity=self.ident[:]),
                 reads=[xnb, self.ident], writes=[tp], mark=(k == 7))
        P.op(P.act, lambda e: e.activation(out=xnT[:, :, col0:col0 + 128], in_=tp[:].rearrange("p (k t) -> p k t", k=8), func=AF.Copy),
             reads=[tp], writes=[xnT])

    def load_weight_bf16(self, wt, src, ncols, kchunks=8, split=1):
        P = self.P
        if getattr(self, "_wstage_pst", None) is not self.pst:
            self._wstage_n = getattr(self, "_wstage_n", 0) + 1
            self._wstage = P.ring("wstg%d_" % self._wstage_n, 2, [128, 1024], F32, self.pst)
            self._wstage_pst = self.pst
            self._wcast_i = 0
        W = 1024
        for k in range(kchunks):
            c0 = 0
            while c0 < ncols:
                w = min(W, ncols - c0)
                stg = self._wstage.next()
                P.dma(P.sp, stg[:, 0:w], src[k * 128:(k + 1) * 128, c0:c0 + w], out_tile=stg)
                which = self._wcast_i % 3
                self._wcast_i += 1
                if which == 0:
                    P.op(P.pool, lambda e, stg=stg, k=k, c0=c0, w=w: e.tensor_copy(out=wt[:, k, c0:c0 + w], in_=stg[:, 0:w]), reads=[stg], writes=[wt])
                elif which == 1:
                    P.op(P.dve, lambda e, stg=stg, k=k, c0=c0, w=w: e.tensor_copy(out=wt[:, k, c0:c0 + w], in_=stg[:, 0:w]), reads=[stg], writes=[wt])
                else:
                    P.op(P.act, lambda e, stg=stg, k=k, c0=c0, w=w: e.activation(out=wt[:, k, c0:c0 + w], in_=stg[:, 0:w], func=AF.Copy), reads=[stg], writes=[wt])
                c0 += w

    def phase_A(self):
        P, pst = self.P, self.pst
        l = self.layer
        ext_in = "M" not in self.phases
        self.modrow = self.dio("modrow", [DEPTH, 2, 6 * D], F32, ext_in, False)
        xres = self.dio("xres", [NT, D], F32, True, False)
        w_fm = self.din("w_in_fm", [D, NFM])
        w_tm = self.din("w_in_tm", [D, NTM])
        g1 = self.din("norm1_g", [1, D])
        wg = self.din("gate_w", [32, 512])
        bg = self.din("gate_b", [1, 512])
        cosT = self.din("cosT", [128, NT])
        sinT = self.din("sinT", [128, NT])
        c_matf = self.din("c_matf", [128, 128])
        c_matb = self.din("c_matb", [128, 128])
        c_misc = self.din("c_misc", [128, 8])
        out_ext = "B" not in self.phases
        fmT = self.dio("fmT", [NFMO, 128, NT], BF16, False, out_ext)
        tmS = self.dio("tmS", [NT, NTM], BF16, False, out_ext)
        spS = self.dio("spS", [NT, 512], F32, False, out_ext)
        ckv = self.dio("ckv", [128, NCH, 256], F32, False, out_ext)
        cdd = self.dio("cdd", [128, NCH, 4], F32, False, out_ext)
        summ = self.dio("summ", [128, 2, 260], F32, False, out_ext)

        wfm = P.sbuf("a_wfm", [128, 8, NFM], BF16, pst)
        wtm = P.sbuf("a_wtm", [128, 8, NTM], BF16, pst)
        wgb = P.sbuf("a_wg", [32, 512], BF16, pst)
        bgb = P.sbuf("a_bg", [1, 512], BF16, pst)
        ones1 = P.sbuf("a_ones", [1, 128], BF16, pst)
        matf = P.sbuf("a_matf", [128, 128], F32, pst)
        matb = P.sbuf("a_matb", [128, 128], F32, pst)
        misc = P.sbuf("a_misc", [128, 8], F32, pst)
        self.epsb = misc
        G1 = P.sbuf("a_G1", [128, D], F32, pst)
        SH1 = P.sbuf("a_SH1", [128, D], F32, pst)
        self.gtile = P.sbuf("a_gt", [128, D], F32, pst)
        xring = P.ring("a_x", 2, [128, 4, D], F32, pst)
        xnTring = P.ring("a_xnT", 2, [128, 8, 512], BF16, pst)
        stat = P.ring("a_stat", 2, [128, 4], F32, pst)
        xn32 = P.ring("a_xn32", 2, [128, D], F32, pst)
        xnb = P.ring("a_xnb", 2, [128, D], BF16, pst)
        stg = P.ring("a_stg", 3, [128, 512], BF16, pst)
        tmp32 = P.ring("a_t32", 3, [128, 512], F32, pst)
        ropec = P.ring("a_rc", 2, [128, 512], F32, pst)
        ropes = P.ring("a_rs", 2, [128, 512], F32, pst)
        lrT = P.ring("a_lrT", 2, [32, 512], BF16, pst)
        tmst = P.ring("a_tmst", 2, [128, NTM], BF16, pst)
        spt = P.ring("a_sp", 2, [128, 512], F32, pst)
        et = P.ring("a_e", 2, [128, 512], F32, pst)
        kh = P.ring("a_kh", 2, [128, 512], BF16, pst)
        kvr = P.ring("a_kvt", 2, [128, 256], F32, pst)
        ddall = P.sbuf("a_ddall", [128, NCH, 4], F32, pst)
        sm = P.sbuf("a_sm", [128, 2, 260], F32, pst)

        P.dma(P.sp, misc[:], c_misc[:], out_tile=misc)
        P.dma(P.sp, matf[:], c_matf[:], out_tile=matf)
        P.dma(P.sp, matb[:], c_matb[:], out_tile=matb)
        P.dma(P.pool, wgb[:], wg[:], out_tile=wgb)
        P.dma(P.pool, bgb[:], bg[:], out_tile=bgb)
        P.op(P.dve, lambda e: e.memset(ones1[:], 1.0), writes=[ones1])
        self.load_mod_tiles([G1, SH1], [1, 0], 0, g1[0:1, :])
        self.load_weight_bf16(wfm, w_fm, NFM)
        self.load_weight_bf16(wtm, w_tm, NTM)

        ps_fm = Ring([self.psb[0], self.psb[1], self.psb[2]])
        ps_ta, ps_tb, ps_z, ps_r = self.psb[3], self.psb[4], self.psb[5], self.psb[6]
        plan = []
        plan += [(0, 128, "pm_a", None), (2 * 128, 128, "pm_b", C_U + 0), (1 * 128, 128, "pm_a", None), (3 * 128, 128, "pm_b", C_U + 1)]
        plan += [(4 * 128, 128, "copy", C_B + 0), (5 * 128, 128, "copy", C_B + 1)]
        for i in range(4):
            plan += [((6 + i) * 128, 128, "rope_a", None), ((10 + i) * 128, 128, "rope_b", C_Q + i)]
        for i in range(2):
            plan += [((14 + i) * 128, 128, "rope_a", None), ((16 + i) * 128, 128, "rope_b", C_K + i)]
        for i in range(2):
            plan += [((18 + i) * 128, 128, "copy", C_GQ + i)]
        for i in range(2):
            plan += [((20 + i) * 128, 128, "copy", C_GK + i)]
        for i in range(2):
            plan += [((22 + i) * 128, 128, "copy", C_GG + i)]
        plan += [(24 * 128, 32, "lr", None)]

        mts = macro_tiles()
        loaded = {}

        def load_mt(i):
            (t0, T, is_ctx) = mts[i]
            xt = xring.next()
            P.dma(P.sp, xt[:, 0:T // 128, :], xres[t0:t0 + T, :].rearrange("(s p) f -> p s f", p=128), out_tile=xt, in_tile=xres)
            rc, rs = ropec.next(), ropes.next()
            P.dma(P.sp, rc[:, 0:T], cosT[:, t0:t0 + T], out_tile=rc)
            P.dma(P.sp, rs[:, 0:T], sinT[:, t0:t0 + T], out_tile=rs)
            loaded[i] = (xt, rc, rs)

        load_mt(0)
        for mi, (t0, T, is_ctx) in enumerate(mts):
            nt = T // 128
            if mi + 1 < len(mts):
                load_mt(mi + 1)
            if is_ctx:
                self.load_mod_tiles([G1, SH1], [1, 0], 1, g1[0:1, :])
            xt, rc, rs = loaded.pop(mi)
            xnT = xnTring.next()
            for s in range(nt):
                x32_ = xn32.next()
                self.norm_transpose(xt, s, G1, SH1, xnT, s * 128, stat.next(), x32_, x32_, xnb.next())
            lr = lrT.next()
            hold = None
            for (c0, M, kind, oc) in plan:
                ps = ps_fm.next()
                for k in range(8):
                    P.op(P.pe, lambda e, k=k, ps=ps: e.matmul(ps[0:M, 0:T], lhsT=wfm[:, k, c0:c0 + M], rhs=xnT[:, k, 0:T], start=(k == 0), stop=(k == 7)),
                         reads=[wfm, xnT], writes=[ps], mark=(k == 7))
                if kind == "copy":
                    sg_ = stg.next()
                    P.op(P.act, lambda e, ps=ps, sg_=sg_: e.activation(out=sg_[:, 0:T], in_=ps[:, 0:T], func=AF.Copy), reads=[ps], writes=[sg_])
                    P.dma(P.sp, fmT[oc][:, t0:t0 + T], sg_[:, 0:T], out_tile=fmT, in_tile=sg_)
                elif kind == "lr":
                    P.op(P.act, lambda e, ps=ps: e.activation(out=lr[:, 0:T], in_=ps[0:32, 0:T], func=AF.Copy), reads=[ps], writes=[lr])
                elif kind == "pm_a":
                    hold = tmp32.next()
                    P.op(P.act, lambda e, ps=ps, h=hold: e.activation(out=h[:, 0:T], in_=ps[:, 0:T], func=AF.Copy), reads=[ps], writes=[hold])
                elif kind == "pm_b":
                    sg_ = stg.next()
                    P.op(P.dve, lambda e, ps=ps, sg_=sg_, h=hold: e.tensor_tensor(out=sg_[:, 0:T], in0=ps[:, 0:T], in1=h[:, 0:T], op=ALU.mult),
                         reads=[ps, hold], writes=[sg_])
                    P.dma(P.sp, fmT[oc][:, t0:t0 + T], sg_[:, 0:T], out_tile=fmT, in_tile=sg_)
                elif kind == "rope_a":
                    hold = tmp32.next()
                    P.op(P.dve, lambda e, ps=ps, h=hold: e.tensor_tensor(out=h[:, 0:T], in0=ps[:, 0:T], in1=rc[:, 0:T], op=ALU.mult),
                         reads=[ps, rc], writes=[hold])
                elif kind == "rope_b":
                    h2 = tmp32.next()
                    sg_ = stg.next()
                    P.op(P.dve, lambda e, ps=ps, h2=h2: e.tensor_tensor(out=h2[:, 0:T], in0=ps[:, 0:T], in1=rs[:, 0:T], op=ALU.mult),
                         reads=[ps, rs], writes=[h2])
                    P.op(P.pool, lambda e, h=hold, h2=h2, sg_=sg_: e.tensor_tensor(out=sg_[:, 0:T], in0=h[:, 0:T], in1=h2[:, 0:T], op=ALU.add),
                         reads=[hold, h2], writes=[sg_])
                    P.dma(P.sp, fmT[oc][:, t0:t0 + T], sg_[:, 0:T], out_tile=fmT, in_tile=sg_)
            for s in range(nt):
                n = (t0 // 128) + s
                tsl = slice(s * 128, (s + 1) * 128)
                for k in range(8):
                    P.op(P.pe, lambda e, k=k: e.matmul(ps_ta[:, 0:512], lhsT=xnT[:, k, tsl], rhs=wtm[:, k, 0:512], start=(k == 0), stop=(k == 7)),
                         reads=[xnT, wtm], writes=[ps_ta], mark=(k == 7))
                pstb_a = ps_tb
                for k in range(8):
                    P.op(P.pe, lambda e, k=k: e.matmul(ps_tb[:, 0:256], lhsT=xnT[:, k, tsl], rhs=wtm[:, k, 512:768], start=(k == 0), stop=(k == 7)),
                         reads=[xnT, wtm], writes=[ps_tb], mark=(k == 7))
                tm = tmst.next()
                P.op(P.act, lambda e, tm=tm: e.activation(out=tm[:, 0:512], in_=ps_ta[:, 0:512], func=AF.Copy), reads=[ps_ta], writes=[tm])
                P.op(P.dve, lambda e, tm=tm: e.tensor_copy(out=tm[:, 512:768], in_=ps_tb[:, 0:256]), reads=[ps_tb], writes=[tm])
                P.dma(P.sp, tmS[n * 128:(n + 1) * 128, :], tm[:], out_tile=tmS, in_tile=tm)
                P.op(P.pe, lambda e: e.matmul(ps_z[:, :], lhsT=lr[:, tsl], rhs=wgb[:, :], start=True, stop=False), reads=[lr, wgb], writes=[ps_z], mark=False)
                P.op(P.pe, lambda e: e.matmul(ps_z[:, :], lhsT=ones1[:, :], rhs=bgb[:, :], start=False, stop=True), reads=[ones1, bgb], writes=[ps_z])
                e1 = et.next()
                sp_ = spt.next()
                P.op(P.act, lambda e, e1=e1: e.activation(out=e1[:], in_=ps_z[:], func=AF.Exp, scale=-1.0), reads=[ps_z], writes=[e1])
                P.op(P.act, lambda e, e1=e1, sp_=sp_: e.activation(out=sp_[:], in_=e1[:], func=AF.Ln, bias=misc[:, 2:3]), reads=[e1, misc], writes=[sp_])
                P.dma(P.sp, spS[n * 128:(n + 1) * 128, :], sp_[:], out_tile=spS, in_tile=sp_)
                P.op(P.pe, lambda e, sp_=sp_: e.matmul(ps_r[:, 0:256], lhsT=matf[:], rhs=sp_[:, 0:256], start=True, stop=True), reads=[matf, sp_], writes=[ps_r], mark=False)
                P.op(P.pe, lambda e, sp_=sp_: e.matmul(ps_r[:, 256:512], lhsT=matb[:], rhs=sp_[:, 256:512], start=True, stop=True), reads=[matb, sp_], writes=[ps_r])
                e2 = et.next()
                P.op(P.act, lambda e, e2=e2: e.activation(out=e2[:], in_=ps_r[:], func=AF.Exp), reads=[ps_r], writes=[e2])
                kh_ = kh.next()
                P.op(P.dve, lambda e, e2=e2, kh_=kh_: e.tensor_tensor(out=kh_[:, 0:256], in0=ps_ta[:, 0:256], in1=e2[:, 0:256], op=ALU.mult), reads=[ps_ta, e2], writes=[kh_])
                P.op(P.dve, lambda e, e2=e2, kh_=kh_: e.tensor_tensor(out=kh_[:, 256:512], in0=ps_ta[:, 0:256], in1=e2[:, 256:512], op=ALU.mult), reads=[ps_ta, e2], writes=[kh_])
                for d in range(2):
                    for hp in range(2):
                        j = d * 2 + hp
                        P.op(P.pe, lambda e, sp_=sp_, d=d, hp=hp, j=j: e.matmul(ps_z[:, j:j + 1], lhsT=sp_[:, d * 256 + hp * 128:d * 256 + (hp + 1) * 128], rhs=misc[:, 0:1], start=True, stop=True),
                             reads=[sp_, misc], writes=[ps_z], mark=(j == 3))
                P.op(P.act, lambda e, n=n: e.activation(out=ddall[:, n, :], in_=ps_z[:, 0:4], func=AF.Exp), reads=[ps_z], writes=[ddall])
                for d in range(2):
                    for h in range(4):
                        po = (h % 2) * 64
                        co = 256 + d * 128 + (h // 2) * 64
                        P.op(P.pe, lambda e, d=d, h=h, po=po, co=co, kh_=kh_, tm=tm: e.matmul(ps_tb[po:po + 64, co:co + 64], lhsT=kh_[:, d * 256 + h * 64:d * 256 + (h + 1) * 64],
                                                                                   rhs=tm[:, 256 + h * 64:256 + (h + 1) * 64], start=True, stop=True),
                             reads=[kh_, tm], writes=[ps_tb], mark=(d == 1 and h == 3))
                kvt = kvr.next()
                P.op(P.act, lambda e, kvt=kvt: e.activation(out=kvt[:], in_=ps_tb[:, 256:512], func=AF.Copy), reads=[ps_tb], writes=[kvt])
                P.dma(P.sp, ckv[:, n, :], kvt[:], out_tile=ckv, in_tile=kvt)
                seg = 1 if is_ctx else 0
                first = (n == 0) or (n == NLAT // 128)
                for d in range(2):
                    for hp in range(2):
                        j = d * 2 + hp
                        sl = slice(d * 128 + hp * 64, d * 128 + (hp + 1) * 64)
                        dj = slice(256 + j, 257 + j)
                        if first:
                            P.op(P.dve, lambda e, sl=sl, kvt=kvt: e.tensor_copy(out=sm[:, seg, sl], in_=kvt[:, sl]), reads=[kvt], writes=[sm])
                            P.op(P.dve, lambda e, dj=dj, j=j, n=n: e.tensor_copy(out=sm[:, seg, dj], in_=ddall[:, n, j:j + 1]), reads=[ddall], writes=[sm])
                        else:
                            if d == 0:
                                P.op(P.dve, lambda e, sl=sl, kvt=kvt, j=j, n=n: e.scalar_tensor_tensor(out=sm[:, seg, sl], in0=sm[:, seg, sl], scalar=ddall[:, n, j:j + 1], in1=kvt[:, sl],
                                                                                                op0=ALU.mult, op1=ALU.add), reads=[sm, ddall, kvt], writes=[sm])
                            else:
                                P.op(P.dve, lambda e, sl=sl, kvt=kvt, dj=dj: e.scalar_tensor_tensor(out=sm[:, seg, sl], in0=kvt[:, sl], scalar=sm[:, seg, dj], in1=sm[:, seg, sl],
                                                                                              op0=ALU.mult, op1=ALU.add), reads=[sm, kvt], writes=[sm])
                            P.op(P.dve, lambda e, dj=dj, j=j, n=n: e.tensor_tensor(out=sm[:, seg, dj], in0=sm[:, seg, dj], in1=ddall[:, n, j:j + 1], op=ALU.mult),
                                 reads=[sm, ddall], writes=[sm])
        P.dma(P.sp, cdd[:], ddall[:], out_tile=cdd, in_tile=ddall)
        P.dma(P.sp, summ[:], sm[:], out_tile=summ, in_tile=sm)

    def chain_summary(self, sm, kvall, ddall):
        P = self.P
        for seg, (lo, hi) in enumerate([(0, NLAT // 128), (NLAT // 128, NCH)]):
            for d in range(2):
                order = list(range(lo, hi)) if d == 0 else list(range(hi - 1, lo - 1, -1))
                for i, n in enumerate(order):
                    for hp in range(2):
                        j = d * 2 + hp
                        sl = slice(d * 128 + hp * 64, d * 128 + (hp + 1) * 64)
                        if i == 0:
                            P.op(P.dve, lambda e, n=n, sl=sl: e.tensor_copy(out=sm[:, seg, sl], in_=kvall[:, n, sl]), reads=[kvall], writes=[sm])
                            P.op(P.dve, lambda e, n=n, j=j: e.tensor_copy(out=sm[:, seg, 256 + j:257 + j], in_=ddall[:, n, j:j + 1]), reads=[ddall], writes=[sm])
                        else:
                            P.op(P.dve, lambda e, n=n, sl=sl, j=j: e.scalar_tensor_tensor(out=sm[:, seg, sl], in0=sm[:, seg, sl], scalar=ddall[:, n, j:j + 1], in1=kvall[:, n, sl],
                                                                                    op0=ALU.mult, op1=ALU.add), reads=[sm, ddall, kvall], writes=[sm])
                            P.op(P.dve, lambda e, n=n, j=j: e.tensor_tensor(out=sm[:, seg, 256 + j:257 + j], in0=sm[:, seg, 256 + j:257 + j], in1=ddall[:, n, j:j + 1], op=ALU.mult),
                                 reads=[sm, ddall], writes=[sm])


def _rope_partner():
    perm = np.zeros(64, np.int64)
    sign = np.zeros(64, np.float32)
    for d in range(64):
        e = d % 32
        if e < 16:
            perm[d] = d + 16
            sign[d] = -1.0
        else:
            perm[d] = d - 16
            sign[d] = 1.0
    return perm, sign


def fm_cols():
    perm, _ = _rope_partner()
    cols = []
    cols += list(range(0, 256))
    cols += list(range(512, 768))
    cols += list(range(256, 512))
    q0 = 768
    cols += list(range(q0, q0 + 512))
    for h in range(8):
        cols += [q0 + h * 64 + int(perm[d]) for d in range(64)]
    k0 = 1280
    for g in range(2):
        cols += list(range(k0 + g * 64, k0 + (g + 1) * 64)) * 2
    for g in range(2):
        cols += [k0 + g * 64 + int(perm[d]) for d in range(64)] * 2
    cols += list(range(1536, 1792))
    cols += list(range(1792, 2048))
    cols += list(range(2304, 2560))
    cols += list(range(2560, 2592))
    assert len(cols) == NFM
    return np.array(cols)


def tm_cols():
    cols = list(range(1792, 2048)) + list(range(2048, 2304))
    v0 = 1408
    for g in range(2):
        cols += list(range(v0 + g * 64, v0 + (g + 1) * 64)) * 2
    assert len(cols) == NTM
    return np.array(cols)


def rope_tables(seg):
    _, sign = _rope_partner()
    t = np.arange(seg * NLAT, (seg + 1) * NLAT)
    row = (t // 64).astype(np.float32)
    col = (t % 64).astype(np.float32)
    inv = (np.float32(10000.0) ** (-np.arange(16, dtype=np.float32) / np.float32(16))).astype(np.float32)
    cosT = np.ones((64, NT), np.float32)
    sinT = np.zeros((64, NT), np.float32)
    for d in range(64):
        pos = row if d < 32 else col
        ang = (pos * inv[d % 16]).astype(np.float32)
        cosT[d, :NLAT] = np.cos(ang)
        sinT[d, :NLAT] = np.sin(ang) * sign[d]
    return np.concatenate([cosT, cosT], 0), np.concatenate([sinT, sinT], 0)


def const_inputs():
    s = np.arange(128)
    c = {}
    c["c_ident"] = np.eye(128, dtype=np.float32)
    g = np.float32(-1.0 / 16.0)
    c["c_matf"] = np.where(s[:, None] > s[None, :], g, 0).astype(np.float32)
    c["c_matb"] = np.where(s[:, None] < s[None, :], g, 0).astype(np.float32)
    misc = np.zeros((128, 8), np.float32)
    misc[:, 0] = g
    misc[:, 1] = EPS
    misc[:, 2] = 1.0
    misc[:, 3] = np.log(np.float32(0.125))
    c["c_misc"] = misc
    return c


def layer_inputs_A(inp, l):
    d = {}
    d["w_in_fm"] = np.ascontiguousarray(inp["w_in"][l][:, fm_cols()])
    d["w_in_tm"] = np.ascontiguousarray(inp["w_in"][l][:, tm_cols()])
    d["norm1_g"] = inp["norm1_g"][l][None, :]
    wg = np.zeros((32, 512), np.float32)
    wg[0:16, 0:256] = inp["gla_gate_w"][l][0]
    wg[16:32, 256:512] = inp["gla_gate_w"][l][1]
    d["gate_w"] = wg
    d["gate_b"] = np.concatenate([inp["gla_gate_b"][l][0], inp["gla_gate_b"][l][1]])[None, :]
    return d


def core_inputs_M(inp, b):
    cT = np.stack([inp["c"][b], inp["c_ctx"]], -1).reshape(8, 128, 2).transpose(1, 0, 2)
    return {"cT": np.ascontiguousarray(cT), "w_mod": inp["w_mod"],
            "b_mod2": np.ascontiguousarray(np.repeat(inp["b_mod"][:, None, :], 2, axis=1))}


def _phase_B(self):
    P, pst = self.P, self.pst
    ext_in = "A" not in self.phases
    fmT = self.dio("fmT", [NFMO, 128, NT], BF16, ext_in, False)
    tmS = self.dio("tmS", [NT, NTM], BF16, ext_in, False)
    spS = self.dio("spS", [NT, 512], F32, ext_in, False)
    ckv = self.dio("ckv", [128, NCH, 256], F32, ext_in, False)
    cdd = self.dio("cdd", [128, NCH, 4], F32, ext_in, False)
    summ = self.dio("summ", [128, 2, 260], F32, ext_in, False)
    fz = self.fused
    sg = self.seg
    summ_all = None if fz else self.din("summ_all", [128, 4, 260])
    chainmask = self.din("chainmask", [128, 8])
    kT_halo = None if fz else self.din("kT_halo", [128, 2, 256], BF16)
    v_halo = None if fz else self.din("v_halo", [128, 2, 256], BF16)
    u_halo = None if fz else self.din("u_halo", [128, 2, 2], BF16)
    if fz:
        fm_full = self.full("fmT", [NSEG, NFMO, 128, NT], BF16, "Internal")
        tm_full = self.full("tmS", [NSEG, NT, NTM], BF16, "Internal")
        sm_full = self.full("summ", [NSEG, 128, 2, 260], F32, "Internal")
    amask = self.din("amask", [128, 4, 512])
    gmask = self.din("gmask", [128, 2, 512])
    c_m2 = self.din("c_m2", [128, 2, 128])
    c_bones = self.din("c_bones", [128, 128])
    c_misc = self.din("c_misc", [128, 8])
    cw = self.din("conv_cw", [128, 2, 3])
    sink = self.din("sink_row", [1, 8])
    ngc = self.din("gla_ng", [128, 1])
    catT = self.dio("catT", [8, 128, NT], BF16, False, "O" not in self.phases)

    misc = P.sbuf("b_misc", [128, 8], F32, pst)
    m2 = P.sbuf("b_m2", [128, 2, 128], F32, pst)
    bones = P.sbuf("b_bones", [128, 128], F32, pst)
    gm = P.sbuf("b_gm", [128, 2, 512], BF16, pst)
    am = P.sbuf("b_am", [128, 4, 512], BF16, pst)
    cwt = P.sbuf("b_cw", [128, 2, 3], F32, pst)
    ng = P.sbuf("b_ng", [128, 1], F32, pst)
    skr = P.sbuf("b_skr", [128, 8], F32, pst)
    onesf = P.sbuf("b_onesf", [128, 128], F32, pst)
    onesb = P.sbuf("b_onesb", [128, 128], BF16, pst)
    sexp = P.sbuf("b_sexp", [128, 2, 512], F32, pst)
    sa = P.sbuf("b_sa", [128, 4, 260], F32, pst)
    so = P.sbuf("b_so", [128, 2, 260], F32, pst)
    cm = P.sbuf("b_cm", [128, 8], F32, pst)
    kva = P.sbuf("b_kva", [128, NCH, 256], F32, pst)
    dda = P.sbuf("b_dda", [128, NCH, 4], F32, pst)
    Sall = P.sbuf("b_Sall", [128, NCH, 256], BF16, pst)
    Sf = P.sbuf("b_Sf", [128, 128], F32, pst)
    Sb = P.sbuf("b_Sb", [128, 128], F32, pst)
    T1f = P.sbuf("b_T1f", [128, 128], F32, pst)
    T1b = P.sbuf("b_T1b", [128, 128], F32, pst)

    P.dma(P.sp, misc[:], c_misc[:], out_tile=misc)
    P.dma(P.sp, m2[:], c_m2[:], out_tile=m2)
    P.dma(P.sp, bones[:], c_bones[:], out_tile=bones)
    P.dma(P.pool, gm[:], gmask[:], out_tile=gm)
    P.dma(P.pool, am[:], amask[:], out_tile=am)
    P.dma(P.sp, cwt[:], cw[:], out_tile=cwt)
    P.dma(P.sp, ng[:], ngc[:], out_tile=ng)
    P.dma(P.sp, skr[:], sink[0:1, :].partition_broadcast(128), out_tile=skr)
    if fz:
        for i in range(NSEG):
            P.dma(P.sp, sa[:, i, :], sm_full[i][:, 0, :], out_tile=sa)
    else:
        P.dma(P.sp, sa[:], summ_all[:], out_tile=sa)
    P.dma(P.sp, so[:], summ[:], out_tile=so, in_tile=summ)
    P.dma(P.sp, cm[:], chainmask[:], out_tile=cm)
    P.dma(P.sp, kva[:], ckv[:], out_tile=kva, in_tile=ckv)
    P.dma(P.sp, dda[:], cdd[:], out_tile=dda, in_tile=cdd)
    P.op(P.dve, lambda e: e.memset(onesf[:], 1.0), writes=[onesf])
    P.op(P.dve, lambda e: e.memset(onesb[:], 1.0), writes=[onesb])
    P.op(P.act, lambda e: e.activation(out=skr[:], in_=skr[:], func=AF.Exp), reads=[skr], writes=[skr])
    for h in range(8):
        P.op(P.dve, lambda e, h=h: e.tensor_scalar(out=sexp[:, h // 4, (h % 4) * 128:(h % 4 + 1) * 128], in0=onesf[:], scalar1=skr[:, h:h + 1], scalar2=None, op0=ALU.mult),
             reads=[onesf, skr], writes=[sexp])

    with self.nc.named_scope("B0_chain"):
        for (E, d, S, T1) in [(P.dve, 0, Sf, T1f), (P.dve, 1, Sb, T1b)]:
            P.op(E, lambda e, d=d, S=S: e.tensor_copy(out=S[:], in_=so[:, 1, d * 128:(d + 1) * 128]), reads=[so], writes=[S])
            order = [0, 1, 2, 3] if d == 0 else [3, 2, 1, 0]
            for i in order:
                for hp in range(2):
                    sl = slice(hp * 64, (hp + 1) * 64)
                    P.op(E, lambda e, i=i, hp=hp, sl=sl, d=d, S=S, T1=T1: e.scalar_tensor_tensor(out=T1[:, sl], in0=S[:, sl], scalar=sa[:, i, 256 + d * 2 + hp:257 + d * 2 + hp],
                                                                                           in1=sa[:, i, d * 128 + hp * 64:d * 128 + (hp + 1) * 64], op0=ALU.mult, op1=ALU.add),
                         reads=[S, sa], writes=[T1])
                P.op(E, lambda e, S=S, T1=T1: e.tensor_tensor(out=T1[:], in0=T1[:], in1=S[:], op=ALU.subtract), reads=[T1, S], writes=[T1])
                P.op(E, lambda e, i=i, d=d, S=S, T1=T1: e.scalar_tensor_tensor(out=S[:], in0=T1[:], scalar=cm[:, d * 4 + i:d * 4 + i + 1], in1=S[:], op0=ALU.mult, op1=ALU.add),
                     reads=[T1, cm, S], writes=[S])
            order = list(range(32)) if d == 0 else list(range(31, -1, -1))
            for n in order:
                P.op(E, lambda e, n=n, d=d, S=S: e.tensor_copy(out=Sall[:, n, d * 128:(d + 1) * 128], in_=S[:]), reads=[S], writes=[Sall])
                for hp in range(2):
                    sl = slice(hp * 64, (hp + 1) * 64)
                    P.op(E, lambda e, n=n, hp=hp, sl=sl, d=d, S=S: e.scalar_tensor_tensor(out=S[:, sl], in0=S[:, sl], scalar=dda[:, n, d * 2 + hp:d * 2 + hp + 1],
                                                                                    in1=kva[:, n, d * 128 + hp * 64:d * 128 + (hp + 1) * 64], op0=ALU.mult, op1=ALU.add),
                         reads=[S, dda, kva], writes=[S])
            zc, oc_, src = (32, 33, 32) if d == 0 else (33, 32, 33)
            P.op(E, lambda e, zc=zc, d=d: e.memset(Sall[:, zc, d * 128:(d + 1) * 128], 0.0), writes=[Sall])
            P.op(E, lambda e, oc_=oc_, src=src, d=d: e.tensor_copy(out=Sall[:, oc_, d * 128:(d + 1) * 128], in_=kva[:, src, d * 128:(d + 1) * 128]), reads=[kva], writes=[Sall])

    with self.nc.named_scope("B1_gla"):
        gqr = P.ring("b_gq", 2, [128, 2, 128], BF16, pst)
        gkr = P.ring("b_gk", 2, [128, 2, 128], BF16, pst)
        ggr = P.ring("b_gg", 2, [128, 2, 128], BF16, pst)
        tmr = P.ring("b_tm", 2, [128, 256], BF16, pst)
        spr = P.ring("b_sp", 2, [128, 512], F32, pst)
        eqr = P.ring("b_eq", 2, [128, 4, 128], F32, pst)
        ekr = P.ring("b_ek", 2, [128, 4, 128], F32, pst)
        qtr = P.ring("b_qt", 2, [128, 4, 128], BF16, pst)
        ktr = P.ring("b_kt", 2, [128, 4, 128], BF16, pst)
        amr = P.ring("b_amr", 2, [128, 2, 512], BF16, pst)
        o32r = P.ring("b_o32", 2, [128, 256], F32, pst)
        sqr = P.ring("b_sq", 2, [128, 256], F32, pst)
        rsr = P.ring("b_rs", 2, [128, 256], F32, pst)
        sgr = P.ring("b_sg", 2, [128, 256], F32, pst)
        ygr = P.ring("b_yg", 2, [128, 2, 128], BF16, pst)
        ps_cb, ps_af, ps_ab, ps_o, ps_m = self.psb[0], self.psb[1], self.psb[2], self.psb[3], self.psb[4]
        for n in range(NCH):
            tk = slice(n * 128, (n + 1) * 128)
            gq, gk, gg, tm, sp_ = gqr.next(), gkr.next(), ggr.next(), tmr.next(), spr.next()
            P.dma(P.sp, gq[:], fmT[C_GQ:C_GQ + 2][:, :, tk].rearrange("c p t -> p c t"), out_tile=gq, in_tile=fmT)
            P.dma(P.sp, gk[:], fmT[C_GK:C_GK + 2][:, :, tk].rearrange("c p t -> p c t"), out_tile=gk, in_tile=fmT)
            P.dma(P.sp, gg[:], fmT[C_GG:C_GG + 2][:, :, tk].rearrange("c p t -> p c t"), out_tile=gg, in_tile=fmT)
            P.dma(P.sp, tm[:], tmS[tk, 256:512], out_tile=tm, in_tile=tmS)
            P.dma(P.sp, sp_[:], spS[tk, :], out_tile=sp_, in_tile=spS)
            for d in range(2):
                for hp in range(2):
                    j = d * 2 + hp
                    P.op(P.pe, lambda e, d=d, hp=hp, j=j, sp_=sp_: e.matmul(ps_cb[:, j * 128:(j + 1) * 128], lhsT=sp_[:, d * 256 + hp * 128:d * 256 + (hp + 1) * 128], rhs=m2[:, d, :], start=True, stop=True),
                         reads=[sp_, m2], writes=[ps_cb], mark=(j == 3))
            eq, ek, qt, kt = eqr.next(), ekr.next(), qtr.next(), ktr.next()
            P.op(P.act, lambda e, eq=eq: e.activation(out=eq[:].rearrange("p a t -> p (a t)"), in_=ps_cb[:], func=AF.Exp, bias=misc[:, 3:4]), reads=[ps_cb, misc], writes=[eq])
            P.op(P.act, lambda e, ek=ek: e.activation(out=ek[:].rearrange("p a t -> p (a t)"), in_=ps_cb[:], func=AF.Exp, scale=-1.0), reads=[ps_cb], writes=[ek])
            for d in range(2):
                P.op(P.dve, lambda e, d=d, qt=qt, gq=gq, eq=eq: e.tensor_tensor(out=qt[:, d * 2:d * 2 + 2, :], in0=gq[:], in1=eq[:, d * 2:d * 2 + 2, :], op=ALU.mult), reads=[gq, eq], writes=[qt])
                P.op(P.pool, lambda e, d=d, kt=kt, gk=gk, ek=ek: e.tensor_tensor(out=kt[:, d * 2:d * 2 + 2, :], in0=gk[:], in1=ek[:, d * 2:d * 2 + 2, :], op=ALU.mult), reads=[gk, ek], writes=[kt])
            am_ = amr.next()
            for d in range(2):
                psa = ps_af if d == 0 else ps_ab
                for h in range(4):
                    pr = slice((h % 2) * 64, (h % 2 + 1) * 64)
                    P.op(P.pe, lambda e, d=d, h=h, pr=pr, psa=psa, kt=kt, qt=qt: e.matmul(psa[:, h * 128:(h + 1) * 128], lhsT=kt[pr, d * 2 + h // 2, :], rhs=qt[pr, d * 2 + h // 2, :], start=True, stop=True),
                         reads=[kt, qt], writes=[psa], mark=(h == 3), serial=True)
                P.op(P.dve, lambda e, d=d, psa=psa, am_=am_: e.tensor_tensor(out=am_[:, d, :], in0=psa[:], in1=gm[:, d, :], op=ALU.mult), reads=[psa, gm], writes=[am_])
            for h in range(4):
                pr = slice((h % 2) * 64, (h % 2 + 1) * 64)
                oc = slice((h // 2) * 128, (h // 2 + 1) * 128)
                for d in range(2):
                    P.op(P.pe, lambda e, d=d, h=h, pr=pr, oc=oc, tm=tm, am_=am_: e.matmul(ps_o[pr, oc], lhsT=tm[:, h * 64:(h + 1) * 64], rhs=am_[:, d, h * 128:(h + 1) * 128], start=(d == 0), stop=False),
                         reads=[tm, am_], writes=[ps_o], mark=False)
                    P.op(P.pe, lambda e, d=d, h=h, pr=pr, oc=oc, qt=qt, n=n: e.matmul(ps_o[pr, oc], lhsT=Sall[pr, n, d * 128 + (h // 2) * 64:d * 128 + (h // 2 + 1) * 64], rhs=qt[pr, d * 2 + h // 2, :], start=False, stop=(d == 1)),
                         reads=[Sall, qt], writes=[ps_o], mark=(d == 1 and h == 3))
            o32, sq, rs_, sg_, yg = o32r.next(), sqr.next(), rsr.next(), sgr.next(), ygr.next()
            P.op(P.act, lambda e, o32=o32: e.activation(out=o32[:], in_=ps_o[:, 0:256], func=AF.Copy), reads=[ps_o], writes=[o32])
            P.op(P.pool, lambda e, o32=o32, sq=sq: e.tensor_tensor(out=sq[:], in0=o32[:], in1=o32[:], op=ALU.mult), reads=[o32], writes=[sq])
            P.op(P.pe, lambda e, sq=sq: e.matmul(ps_m[:, 0:256], lhsT=bones[:], rhs=sq[:], start=True, stop=True), reads=[bones, sq], writes=[ps_m])
            P.op(P.act, lambda e, rs_=rs_: e.activation(out=rs_[:], in_=ps_m[:, 0:256], func=AF.Sqrt, bias=misc[:, 1:2]), reads=[ps_m, misc], writes=[rs_])
            P.op(P.dve, lambda e, rs_=rs_: e.reciprocal(out=rs_[:], in_=rs_[:]), reads=[rs_], writes=[rs_])
            P.op(P.act, lambda e, sg_=sg_, gg=gg: e.activation(out=sg_[:], in_=gg[:].rearrange("p a t -> p (a t)"), func=AF.Silu), reads=[gg], writes=[sg_])
            P.op(P.dve, lambda e, o32=o32, rs_=rs_: e.tensor_tensor(out=o32[:], in0=o32[:], in1=rs_[:], op=ALU.mult), reads=[o32, rs_], writes=[o32])
            P.op(P.dve, lambda e, o32=o32, sg_=sg_, yg=yg: e.scalar_tensor_tensor(out=yg[:].rearrange("p a t -> p (a t)"), in0=o32[:], scalar=ng[:, 0:1], in1=sg_[:], op0=ALU.mult, op1=ALU.mult),
                 reads=[o32, ng, sg_], writes=[yg])
            P.dma(P.sp, catT[6:8][:, :, tk].rearrange("c p t -> p c t"), yg[:], out_tile=catT, in_tile=yg)

    with self.nc.named_scope("B2_conv"):
        uext = P.sbuf("b_uext", [128, 2, NLAT + 2], BF16, pst)
        uctx = P.sbuf("b_uctx", [128, 2, NCTX + 2], BF16, pst)
        uh = P.sbuf("b_uh", [128, 2, 2], BF16, pst)
        btr = P.ring("b_bt", 2, [128, 512], BF16, pst)
        accr = P.ring("b_acc", 2, [128, 512], F32, pst)
        cstg = P.ring("b_cstg", 2, [128, 512], BF16, pst)
        if fz:
            P.op(P.pool, lambda e: e.memset(uh[:], 0.0), writes=[uh])
            for c in range(2):
                if sg > 0:
                    P.dma(P.sp, uh[:, c, 0:1], fm_full[sg - 1][C_U + c][:, NLAT - 1:NLAT], out_tile=uh, allow_slow_non_contiguous=True)
                if sg < NSEG - 1:
                    P.dma(P.sp, uh[:, c, 1:2], fm_full[sg + 1][C_U + c][:, 0:1], out_tile=uh, allow_slow_non_contiguous=True)
        else:
            P.dma(P.sp, uh[:], u_halo[:], out_tile=uh)
        P.op(P.pool, lambda e: e.memset(uctx[:], 0.0), writes=[uctx])
        for c in range(2):
            P.dma(P.sp, uext[:, c, 1:NLAT + 1], fmT[C_U + c][:, 0:NLAT], out_tile=uext, in_tile=fmT)
            P.dma(P.sp, uctx[:, c, 1:NCTX + 1], fmT[C_U + c][:, NLAT:NT], out_tile=uctx, in_tile=fmT)
        for c in range(2):
            P.op(P.pool, lambda e, c=c: e.tensor_copy(out=uext[:, c, 0:1], in_=uh[:, c, 0:1]), reads=[uh], writes=[uext])
            P.op(P.pool, lambda e, c=c: e.tensor_copy(out=uext[:, c, NLAT + 1:NLAT + 2], in_=uh[:, c, 1:2]), reads=[uh], writes=[uext])
        for (t0, T, is_ctx) in macro_tiles():
            for c in range(2):
                src = uctx if is_ctx else uext
                o0 = 0 if is_ctx else t0
                bt, acc, st_ = btr.next(), accr.next(), cstg.next()
                P.dma(P.sp, bt[:, 0:T], fmT[C_B + c][:, t0:t0 + T], out_tile=bt, in_tile=fmT)
                P.op(P.act, lambda e, c=c, acc=acc, src=src, o0=o0, T=T: e.activation(out=acc[:, 0:T], in_=src[:, c, o0:o0 + T], func=AF.Copy, scale=cwt[:, c, 0:1]), reads=[src, cwt], writes=[acc])
                for jj in (1, 2):
                    P.op(P.dve, lambda e, c=c, acc=acc, src=src, o0=o0, T=T, jj=jj: e.scalar_tensor_tensor(out=acc[:, 0:T], in0=src[:, c, o0 + jj:o0 + jj + T], scalar=cwt[:, c, jj:jj + 1], in1=acc[:, 0:T],
                                                                                                   op0=ALU.mult, op1=ALU.add), reads=[src, cwt, acc], writes=[acc])
                P.op(P.pool, lambda e, acc=acc, bt=bt, st_=st_, T=T: e.tensor_tensor(out=st_[:, 0:T], in0=acc[:, 0:T], in1=bt[:, 0:T], op=ALU.mult), reads=[acc, bt], writes=[st_])
                P.dma(P.sp, catT[c][:, t0:t0 + T], st_[:, 0:T], out_tile=catT, in_tile=st_)

    with self.nc.named_scope("B3_attn"):
        kTall = P.sbuf("b_kT", [128, 2, NLAT + 256], BF16, pst)
        kTc = P.sbuf("b_kTc", [128, 2, NCTX], BF16, pst)
        vall = P.sbuf("b_v", [128, 34, 256], BF16, pst)
        vctx = P.sbuf("b_vc", [128, 2, 256], BF16, pst)
        qr_ = P.ring("b_q", 2, [128, 4, 128], BF16, pst)
        ptr = P.ring("b_pt", 2, [128, 5, 512], BF16, pst)
        denr = P.ring("b_den", 2, [128, 512], F32, pst)
        astg = P.ring("b_astg", 2, [128, 2, 128], BF16, pst)
        for g in range(2):
            P.dma(P.sp, kTall[:, g, 128:128 + NLAT], fmT[C_K + g][:, 0:NLAT], out_tile=kTall, in_tile=fmT)
            P.dma(P.sp, kTc[:, g, :], fmT[C_K + g][:, NLAT:NT], out_tile=kTc, in_tile=fmT)
        if fz:
            for g in range(2):
                if sg > 0:
                    P.dma(P.sp, kTall[:, g, 0:128], fm_full[sg - 1][C_K + g][:, NLAT - 128:NLAT], out_tile=kTall)
                else:
                    P.op(P.pool, lambda e, g=g: e.memset(kTall[:, g, 0:128], 0.0), writes=[kTall])
                if sg < NSEG - 1:
                    P.dma(P.sp, kTall[:, g, 128 + NLAT:256 + NLAT], fm_full[sg + 1][C_K + g][:, 0:128], out_tile=kTall)
                else:
                    P.op(P.pool, lambda e, g=g: e.memset(kTall[:, g, 128 + NLAT:256 + NLAT], 0.0), writes=[kTall])
        else:
            P.dma(P.sp, kTall[:, :, 0:128], kT_halo[:, :, 0:128], out_tile=kTall)
            P.dma(P.sp, kTall[:, :, 128 + NLAT:256 + NLAT], kT_halo[:, :, 128:256], out_tile=kTall)
        P.dma(P.sp, vall[:, 1:33, :], tmS[0:NLAT, 512:768].rearrange("(n p) c -> p n c", p=128), out_tile=vall, in_tile=tmS)
        if fz:
            if sg > 0:
                P.dma(P.sp, vall[:, 0, :], tm_full[sg - 1][NLAT - 128:NLAT, 512:768], out_tile=vall)
            else:
                P.op(P.pool, lambda e: e.memset(vall[:, 0, :], 0.0), writes=[vall])
            if sg < NSEG - 1:
                P.dma(P.sp, vall[:, 33, :], tm_full[sg + 1][0:128, 512:768], out_tile=vall)
            else:
                P.op(P.pool, lambda e: e.memset(vall[:, 33, :], 0.0), writes=[vall])
        else:
            P.dma(P.sp, vall[:, 0, :], v_halo[:, 0, :], out_tile=vall)
            P.dma(P.sp, vall[:, 33, :], v_halo[:, 1, :], out_tile=vall)
        P.dma(P.sp, vctx[:], tmS[NLAT:NT, 512:768].rearrange("(n p) c -> p n c", p=128), out_tile=vctx, in_tile=tmS)
        ps_s = [self.psb[i] for i in range(5)]
        ps_pv, ps_dn = self.psb[5], self.psb[6]
        for i in range(NCH):
            is_ctx = i >= 32
            tk = slice(i * 128, (i + 1) * 128)
            q = qr_.next()
            P.dma(P.sp, q[:], fmT[C_Q:C_Q + 4][:, :, tk].rearrange("c p t -> p c t"), out_tile=q, in_tile=fmT)
            for g in range(2):
                keys = [(kTc, slice(0, 128), vctx, 0, None), (kTc, slice(128, 256), vctx, 1, None)]
                if not is_ctx:
                    keys.append((kTall, slice(i * 128, (i + 1) * 128), vall, i, 0 if i == 0 else 1))
                    keys.append((kTall, slice((i + 1) * 128, (i + 2) * 128), vall, i + 1, None))
                    keys.append((kTall, slice((i + 2) * 128, (i + 3) * 128), vall, i + 2, 3 if i == 31 else 2))
                pt = ptr.next()
                nk = len(keys)
                for kc, (kt_, ks, vt_, vi, mk) in enumerate(keys):
                    for hh in range(4):
                        pr = slice((hh % 2) * 64, (hh % 2 + 1) * 64)
                        P.op(P.pe, lambda e, kc=kc, hh=hh, pr=pr, kt_=kt_, ks=ks, q=q, g=g: e.matmul(ps_s[kc][:, hh * 128:(hh + 1) * 128], lhsT=kt_[pr, g, ks], rhs=q[pr, 2 * g + hh // 2, :], start=True, stop=True),
                             reads=[kt_, q], writes=[ps_s[kc]], mark=(hh == 3), serial=True)
                    P.op(P.act, lambda e, kc=kc, pt=pt: e.activation(out=pt[:, kc, :], in_=ps_s[kc][:], func=AF.Exp, scale=0.125), reads=[ps_s[kc]], writes=[pt])
                    if mk is not None:
                        P.op(P.dve, lambda e, kc=kc, pt=pt, mk=mk: e.tensor_tensor(out=pt[:, kc, :], in0=pt[:, kc, :], in1=am[:, mk, :], op=ALU.mult), reads=[pt, am], writes=[pt])
                for kc, (kt_, ks, vt_, vi, mk) in enumerate(keys):
                    P.op(P.pe, lambda e, kc=kc, vt_=vt_, vi=vi, pt=pt, g=g: e.matmul(ps_pv[:], lhsT=vt_[:, vi, g * 128:(g + 1) * 128], rhs=pt[:, kc, :], start=(kc == 0), stop=(kc == nk - 1)),
                         reads=[vt_, pt], writes=[ps_pv], mark=(kc == nk - 1))
                for kc in range(nk):
                    P.op(P.pe, lambda e, kc=kc, pt=pt: e.matmul(ps_dn[:], lhsT=onesb[:], rhs=pt[:, kc, :], start=(kc == 0), stop=(kc == nk - 1)),
                         reads=[onesb, pt], writes=[ps_dn], mark=(kc == nk - 1))
                den, st_ = denr.next(), astg.next()
                P.op(P.dve, lambda e, den=den, g=g: e.tensor_tensor(out=den[:], in0=ps_dn[:], in1=sexp[:, g, :], op=ALU.add), reads=[ps_dn, sexp], writes=[den])
                P.op(P.dve, lambda e, den=den: e.reciprocal(out=den[:], in_=den[:]), reads=[den], writes=[den])
                for hh in range(4):
                    pr = slice((hh % 2) * 64, (hh % 2 + 1) * 64)
                    P.op(P.dve, lambda e, hh=hh, pr=pr, den=den, st_=st_: e.tensor_tensor(out=st_[pr, hh // 2, :], in0=ps_pv[pr, hh * 128:(hh + 1) * 128], in1=den[pr, hh * 128:(hh + 1) * 128], op=ALU.mult),
                         reads=[ps_pv, den], writes=[st_])
                P.dma(P.sp, catT[2 + 2 * g:4 + 2 * g][:, :, tk].rearrange("c p t -> p c t"), st_[:], out_tile=catT, in_tile=st_)


Builder.phase_B = _phase_B


def _phase_O(self):
    P, pst = self.P, self.pst
    l = self.layer
    catT = self.dio("catT", [8, 128, NT], BF16, "B" not in self.phases, False)
    xres = self.dio("xres", [NT, D], F32, True, False)
    self.modrow = self.dio("modrow", [DEPTH, 2, 6 * D], F32, True, False)
    w_out = self.din("w_out", [D, D])
    x1 = self.dio("x1", [NT, D], F32, False, "F" not in self.phases)
    wo = P.sbuf("o_w", [128, 8, D], BF16, pst)
    GT = P.sbuf("o_gt", [128, D], F32, pst)
    catr = P.ring("o_cat", 2, [128, 8, 512], BF16, pst)
    xr = P.ring("o_x", 2, [128, 4, D], F32, pst)
    tmpr = P.ring("o_tmp", 2, [128, 512], F32, pst)
    self.load_mod_tiles([GT], [2], 0, None)
    self.load_weight_bf16(wo, w_out, D)
    pss = Ring([self.psb[0], self.psb[1], self.psb[2], self.psb[3]])
    for (t0, T, is_ctx) in macro_tiles():
        nt = T // 128
        if is_ctx:
            self.load_mod_tiles([GT], [2], 1, None)
        cat, xt = catr.next(), xr.next()
        P.dma(P.sp, cat[:, :, 0:T], catT[:, :, t0:t0 + T].rearrange("c p t -> p c t"), out_tile=cat, in_tile=catT)
        P.dma(P.sp, xt[:, 0:nt, :], xres[t0:t0 + T, :].rearrange("(s p) f -> p s f", p=128), out_tile=xt, in_tile=xres)
        for s in range(nt):
            for hf in range(2):
                ps = pss.next()
                cs = slice(hf * 512, (hf + 1) * 512)
                for mc in range(8):
                    P.op(P.pe, lambda e, mc=mc, ps=ps, cs=cs, s=s, cat=cat: e.matmul(ps[:], lhsT=cat[:, mc, s * 128:(s + 1) * 128], rhs=wo[:, mc, cs], start=(mc == 0), stop=(mc == 7)),
                         reads=[cat, wo], writes=[ps], mark=(mc == 7))
                tmp = tmpr.next()
                P.op(P.dve, lambda e, ps=ps, cs=cs, tmp=tmp: e.tensor_tensor(out=tmp[:], in0=ps[:], in1=GT[:, cs], op=ALU.mult), reads=[ps, GT], writes=[tmp])
                P.op(P.pool, lambda e, cs=cs, s=s, tmp=tmp, xt=xt: e.tensor_tensor(out=xt[:, s, cs], in0=xt[:, s, cs], in1=tmp[:], op=ALU.add), reads=[xt, tmp], writes=[xt])
        P.dma(P.sp, x1[t0:t0 + T, :].rearrange("(s p) f -> p s f", p=128), xt[:, 0:nt, :], out_tile=x1, in_tile=xt)


def _phase_F(self):
    P, pst = self.P, self.pst
    l = self.layer
    x1 = self.dio("x1", [NT, D], F32, "O" not in self.phases, False)
    self.modrow = self.dio("modrow", [DEPTH, 2, 6 * D], F32, True, False)
    w_up = self.din("w_up", [D, 2 * DFF])
    w_dn = self.din("w_down", [DFF, D])
    g2 = self.din("norm2_g", [1, D])
    c_misc = self.din("c_misc", [128, 8])
    xout = self.dout("xout", [NT, D])
    wu = P.sbuf("f_wu", [128, 8, 2 * DFF], BF16, pst)
    wd = P.sbuf("f_wd", [128, 22, D], BF16, pst)
    misc = P.sbuf("f_misc", [128, 8], F32, pst)
    self.epsb = misc
    G2 = P.sbuf("f_G2", [128, D], F32, pst)
    SH2 = P.sbuf("f_SH2", [128, D], F32, pst)
    GT2 = P.sbuf("f_GT2", [128, D], F32, pst)
    self.gtile = P.sbuf("f_gt", [128, D], F32, pst)
    xr = P.ring("f_x", 2, [128, 2, D], F32, pst)
    xnTr = P.ring("f_xnT", 2, [128, 8, 256], BF16, pst)
    stat = P.ring("f_stat", 2, [128, 4], F32, pst)
    xn32 = P.sbuf("f_xn32", [128, D], F32, pst)
    xnb = P.ring("f_xnb", 2, [128, D], BF16, pst)
    hT = P.sbuf("f_hT", [128, 22, 256], BF16, pst)
    sar = P.ring("f_sa", 2, [128, 256], F32, pst)
    tmpr = P.ring("f_tmp", 2, [128, 512], F32, pst)
    P.dma(P.sp, misc[:], c_misc[:], out_tile=misc)
    self.load_mod_tiles([G2, SH2, GT2], [4, 3, 5], 0, g2[0:1, :])
    self.load_weight_bf16(wu, w_up, 2 * DFF, split=2)
    self.load_weight_bf16(wd, w_dn, D, kchunks=22)
    if self.last:
        fg = self.din("final_g", [1, D])
        FG = self.gtile
        P.dma(P.sp, FG[:], fg[0:1, :].partition_broadcast(128), out_tile=FG)
    psab = Ring([self.psb[0], self.psb[1], self.psb[2], self.psb[3]])
    psy = Ring([self.psb[4], self.psb[5]])
    T = 256
    mts = [(i * T, T, i * T >= NLAT) for i in range(NT // T)]
    loaded = {}

    def load_mt(i):
        t0 = mts[i][0]
        xt = xr.next()
        P.dma(P.sp, xt[:], x1[t0:t0 + T, :].rearrange("(s p) f -> p s f", p=128), out_tile=xt, in_tile=x1)
        loaded[i] = xt

    load_mt(0)
    for mi, (t0, T_, is_ctx) in enumerate(mts):
        if mi + 1 < len(mts):
            load_mt(mi + 1)
        if is_ctx:
            self.load_mod_tiles([G2, SH2, GT2], [4, 3, 5], 1, g2[0:1, :])
        xt = loaded.pop(mi)
        xnT = xnTr.next()
        for s in range(2):
            self.norm_transpose(xt, s, G2, SH2, xnT, s * 128, stat.next(), xn32, xn32, xnb.next())
        for fc in range(22):
            pa, pb = psab.next(), psab.next()
            for (pp, c0) in ((pa, fc * 128), (pb, DFF + fc * 128)):
                for k in range(8):
                    P.op(P.pe, lambda e, k=k, pp=pp, c0=c0, xnT=xnT: e.matmul(pp[:, 0:T], lhsT=wu[:, k, c0:c0 + 128], rhs=xnT[:, k, :], start=(k == 0), stop=(k == 7)),
                         reads=[wu, xnT], writes=[pp], mark=(k == 7))
            sa = sar.next()
            P.op(P.act, lambda e, pa=pa, sa=sa: e.activation(out=sa[:], in_=pa[:, 0:T], func=AF.Silu), reads=[pa], writes=[sa])
            P.op(P.dve, lambda e, pb=pb, sa=sa, fc=fc: e.tensor_tensor(out=hT[:, fc, :], in0=pb[:, 0:T], in1=sa[:], op=ALU.mult), reads=[pb, sa], writes=[hT])
        for s in range(2):
            for hf in range(2):
                cs = slice(hf * 512, (hf + 1) * 512)
                ps = psy.next()
                for fc in range(22):
                    P.op(P.pe, lambda e, fc=fc, ps=ps, cs=cs, s=s: e.matmul(ps[:], lhsT=hT[:, fc, s * 128:(s + 1) * 128], rhs=wd[:, fc, cs], start=(fc == 0), stop=(fc == 21)),
                         reads=[hT, wd], writes=[ps], mark=(fc == 21))
                tmp = tmpr.next()
                P.op(P.dve, lambda e, ps=ps, cs=cs, tmp=tmp: e.tensor_tensor(out=tmp[:], in0=ps[:], in1=GT2[:, cs], op=ALU.mult), reads=[ps, GT2], writes=[tmp])
                P.op(P.pool, lambda e, cs=cs, s=s, tmp=tmp, xt=xt: e.tensor_tensor(out=xt[:, s, cs], in0=xt[:, s, cs], in1=tmp[:], op=ALU.add), reads=[xt, tmp], writes=[xt])
            if self.last and not is_ctx:
                st_ = stat.next()
                P.op(P.act, lambda e, s=s, xt=xt, st_=st_: e.activation(out=xn32[:], in_=xt[:, s, :], func=AF.Square, accum_out=st_[:, 0:1]), reads=[xt], writes=[xn32, st_])
                P.op(P.act, lambda e, st_=st_: e.activation(out=st_[:, 1:2], in_=st_[:, 0:1], func=AF.Sqrt, scale=1.0 / D, bias=misc[:, 1:2]), reads=[st_, misc], writes=[st_])
                P.op(P.dve, lambda e, st_=st_: e.reciprocal(out=st_[:, 2:3], in_=st_[:, 1:2]), reads=[st_], writes=[st_])
                P.op(P.dve, lambda e, s=s, xt=xt, st_=st_: e.scalar_tensor_tensor(out=xt[:, s, :], in0=xt[:, s, :], scalar=st_[:, 2:3], in1=FG[:], op0=ALU.mult, op1=ALU.mult),
                     reads=[xt, st_, FG], writes=[xt])
        P.dma(P.sp, xout[t0:t0 + T, :].rearrange("(s p) f -> p s f", p=128), xt[:], out_tile=xout, in_tile=xt)


Builder.phase_O = _phase_O
Builder.phase_F = _phase_F


def const_inputs_B():
    s = np.arange(128)
    c = {}
    g = np.float32(-1.0 / 16.0)
    m2 = np.zeros((128, 2, 128), np.float32)
    m2[:, 0, :] = np.where(s[:, None] <= s[None, :], g, 0)
    m2[:, 1, :] = np.where(s[:, None] >= s[None, :], g, 0)
    c["c_m2"] = m2
    bo = np.zeros((128, 128), np.float32)
    bo[0:64, 0:64] = 1.0 / 64
    bo[64:128, 64:128] = 1.0 / 64
    c["c_bones"] = bo
    gm = np.zeros((128, 2, 4, 128), np.float32)
    gm[:, 0] = (s[:, None] <= s[None, :]).astype(np.float32)[:, None, :]
    gm[:, 1] = (s[:, None] >= s[None, :]).astype(np.float32)[:, None, :]
    c["gmask"] = gm.reshape(128, 2, 512)
    return c


def attn_masks(seg):
    s = np.arange(128)
    mp = (s[:, None] >= s[None, :]).astype(np.float32)
    mn = (s[:, None] <= s[None, :]).astype(np.float32)
    am = np.zeros((128, 4, 4, 128), np.float32)
    am[:, 0] = (mp if seg > 0 else np.zeros_like(mp))[:, None, :]
    am[:, 1] = mp[:, None, :]
    am[:, 2] = mn[:, None, :]
    am[:, 3] = (mn if seg < NSEG - 1 else np.zeros_like(mn))[:, None, :]
    return am.reshape(128, 4, 512)


def layer_inputs_B(inp, l):
    d = {}
    d["conv_cw"] = np.ascontiguousarray(inp["conv_w"][l].reshape(3, 2, 128).transpose(2, 1, 0))
    d["sink_row"] = inp["attn_sink"][l][None, :]
    d["gla_ng"] = np.concatenate([inp["gla_norm_g"][l], inp["gla_norm_g"][l]])[:, None]
    d["w_out"] = inp["w_out"][l]
    d["w_up"] = inp["w_up"][l]
    d["w_down"] = inp["w_down"][l]
    d["norm2_g"] = inp["norm2_g"][l][None, :]
    return d


_PROGS = {}


def get_prog(phases, last=False):
    key = (tuple(phases), last)
    if key not in _PROGS:
        b = Builder(list(phases), last=last)
        _PROGS[key] = b.build()
    return _PROGS[key]


def kernel(x, c, ctx, c_ctx, w_mod, b_mod, norm1_g, norm2_g, w_in, conv_w, attn_sink,
           gla_gate_w, gla_gate_b, gla_norm_g, w_out, w_up, w_down, final_norm_g):
    inp = dict(x=x, c=c, ctx=ctx, c_ctx=c_ctx, w_mod=w_mod, b_mod=b_mod, norm1_g=norm1_g, norm2_g=norm2_g, w_in=w_in,
               conv_w=conv_w, attn_sink=attn_sink, gla_gate_w=gla_gate_w, gla_gate_b=gla_gate_b, gla_norm_g=gla_norm_g,
               w_out=w_out, w_up=w_up, w_down=w_down, final_norm_g=final_norm_g)
    inp = {k: np.ascontiguousarray(np.asarray(v, dtype=np.float32)) for k, v in inp.items()}
    ncore = 8
    ids = list(range(ncore))
    consts = const_inputs()
    constsB = const_inputs_B()
    ropes = [rope_tables(j) for j in range(NSEG)]
    if FUSED:
        return kernel_fused(inp, consts, constsB, ropes)
    resM = run_bass_kernel_spmd(get_prog(["M"]), [dict(c_ident=consts["c_ident"], **core_inputs_M(inp, cid // 4)) for cid in ids], core_ids=ids)
    modrows = [np.asarray(r["modrow"]) for r in resM.results]
    xres = [np.concatenate([inp["x"][cid // 4, (cid % 4) * NLAT:(cid % 4 + 1) * NLAT], inp["ctx"][cid // 4]], 0) for cid in ids]
    for l in range(DEPTH):
        la = layer_inputs_A(inp, l)
        mapsA = []
        for cid in ids:
            m = dict(consts)
            m.update(la)
            m["modrow"] = np.ascontiguousarray(np.roll(modrows[cid], -l, axis=0))
            m["xres"] = xres[cid]
            m["cosT"], m["sinT"] = ropes[cid % 4]
            mapsA.append(m)
        resA = run_bass_kernel_spmd(get_prog(["A"]), mapsA, core_ids=ids)
        RA = [{k: np.asarray(v) for k, v in r.items()} for r in resA.results]
        lb = layer_inputs_B(inp, l)
        mapsB = []
        for cid in ids:
            b, j = cid // 4, cid % 4
            ra = RA[cid]
            m = dict(c_ident=consts["c_ident"], c_misc=consts["c_misc"])
            m.update(constsB)
            m.update(lb)
            for k in ("fmT", "tmS", "spS", "ckv", "cdd", "summ"):
                m[k] = ra[k]
            m["summ_all"] = np.ascontiguousarray(np.stack([RA[b * 4 + i]["summ"][:, 0, :] for i in range(4)], axis=1))
            cmk = np.zeros((128, 8), np.float32)
            for i in range(4):
                cmk[:, i] = 1.0 if i < j else 0.0
                cmk[:, 4 + i] = 1.0 if i > j else 0.0
            m["chainmask"] = cmk
            kh = np.zeros((128, 2, 256), ra["fmT"].dtype)
            vh = np.zeros((128, 2, 256), ra["tmS"].dtype)
            uh = np.zeros((128, 2, 2), ra["fmT"].dtype)
            if j > 0:
                rp = RA[cid - 1]
                kh[:, :, 0:128] = rp["fmT"][C_K:C_K + 2, :, NLAT - 128:NLAT].transpose(1, 0, 2)
                vh[:, 0, :] = rp["tmS"][NLAT - 128:NLAT, 512:768]
                uh[:, :, 0] = rp["fmT"][C_U:C_U + 2, :, NLAT - 1].T
            if j < NSEG - 1:
                rn = RA[cid + 1]
                kh[:, :, 128:256] = rn["fmT"][C_K:C_K + 2, :, 0:128].transpose(1, 0, 2)
                vh[:, 1, :] = rn["tmS"][0:128, 512:768]
                uh[:, :, 1] = rn["fmT"][C_U:C_U + 2, :, 0].T
            m["kT_halo"], m["v_halo"], m["u_halo"] = kh, vh, uh
            m["amask"] = attn_masks(j)
            m["xres"] = xres[cid]
            m["modrow"] = np.ascontiguousarray(np.roll(modrows[cid], -l, axis=0))
            last = (l == DEPTH - 1)
            if last:
                m["final_g"] = inp["final_norm_g"][None, :]
            mapsB.append(m)
        resB = run_bass_kernel_spmd(get_prog(["B", "O", "F"], last=(l == DEPTH - 1)), mapsB, core_ids=ids)
        xres = [np.asarray(r["xout"]) for r in resB.results]
    out = np.zeros((NB, SEQ, D), np.float32)
    for cid in ids:
        out[cid // 4, (cid % 4) * NLAT:(cid % 4 + 1) * NLAT] = xres[cid][:NLAT]
    return out


def kernel_fused(inp, consts, constsB, ropes):
    ids = list(range(8))
    shared = dict(consts)
    shared.update(constsB)
    las = [layer_inputs_A(inp, l) for l in range(DEPTH)]
    lbs = [layer_inputs_B(inp, l) for l in range(DEPTH)]
    for k in las[0]:
        shared[k] = np.ascontiguousarray(np.stack([las[l][k] for l in range(DEPTH)], 0))
    for k in lbs[0]:
        shared[k] = np.ascontiguousarray(np.stack([lbs[l][k] for l in range(DEPTH)], 0))
    shared["cosT"] = np.ascontiguousarray(np.stack([ropes[j][0] for j in range(NSEG)], 0))
    shared["sinT"] = np.ascontiguousarray(np.stack([ropes[j][1] for j in range(NSEG)], 0))
    shared["amask"] = np.ascontiguousarray(np.stack([attn_masks(j) for j in range(NSEG)], 0))
    cmk = np.zeros((NSEG, 128, 8), np.float32)
    for j in range(NSEG):
        for i in range(4):
            cmk[j, :, i] = 1.0 if i < j else 0.0
            cmk[j, :, 4 + i] = 1.0 if i > j else 0.0
    shared["chainmask"] = cmk
    shared["final_g"] = inp["final_norm_g"][None, :]
    per_b = []
    for b in range(NB):
        m = dict(shared)
        m.update(core_inputs_M(inp, b))
        m["xres"] = np.ascontiguousarray(np.stack(
            [np.concatenate([inp["x"][b, j * NLAT:(j + 1) * NLAT], inp["ctx"][b]], 0) for j in range(NSEG)], 0))
        per_b.append(m)
    key = ("fused",)
    if key not in _PROGS:
        _PROGS[key] = Builder([], fused=True).build()
    res = run_bass_kernel_spmd(_PROGS[key], [per_b[c // 4] for c in ids], core_ids=ids)
    out = np.zeros((NB, SEQ, D), np.float32)
    for b in range(NB):
        y = np.asarray(res.results[4 * b]["yout"])
        for j in range(NSEG):
            out[b, j * NLAT:(j + 1) * NLAT] = y[j, :NLAT]
    return out
```
